# Optimizing a Trainium2 kernel written in Bass

```python
import math
import jax, jax.numpy as jnp
from jax import lax
import numpy as np

D_MODEL = 4096
BATCH = 2
SEQ = 8192
DEPTH = 4

CHUNK = 64
N_A_LAYERS = DEPTH // 2
N_B_LAYERS = DEPTH - N_A_LAYERS
EPS = 1e-6

SSM_EXPAND = 2
SSM_D_INNER = SSM_EXPAND * D_MODEL
SSM_HEADDIM = 64
SSM_HEADS = SSM_D_INNER // SSM_HEADDIM
SSM_GROUPS = 8
SSM_HEADS_PER_GROUP = SSM_HEADS // SSM_GROUPS
SSM_STATE = 128
SSM_CONV = 4
SSM_SCAN_CHUNK = CHUNK
SSM_CONV_DIM = SSM_D_INNER + 2 * SSM_GROUPS * SSM_STATE
SSM_IN_DIM = SSM_D_INNER + SSM_CONV_DIM + SSM_HEADS

ATT_HEADDIM = 128
ATT_HEADS = D_MODEL // ATT_HEADDIM
ATT_D = ATT_HEADS * ATT_HEADDIM
ATT_LEFT_CHUNKS = 8
ATT_BAND = (ATT_LEFT_CHUNKS + 1) * CHUNK
REL_MAX = 128
REL_BUCKETS = 2 * REL_MAX + 1

kernel_name = "yoco_mamba2_chunked_relattn_trunk"


def rms_norm(x, g):
    xf = x.astype(jnp.float32)
    var = jnp.mean(xf * xf, axis=-1, keepdims=True)
    return (xf * lax.rsqrt(var + EPS)).astype(x.dtype) * g


def ada_modulation(c, w, b):
    mod = jax.nn.silu(c) @ w + b
    shift, scale, gate = jnp.split(mod, 3, axis=-1)
    return shift[:, None], scale[:, None], gate[:, None]


def causal_depthwise_conv(u, w, b):
    K = w.shape[0]
    S = u.shape[1]
    up = jnp.pad(u, ((0, 0), (K - 1, 0), (0, 0)))
    return b + sum(up[:, k:k + S] * w[k] for k in range(K))


def ssd_chunked_scan(xh, dt, A, Bm, Cm):
    Bsz, S = xh.shape[:2]
    L = SSM_SCAN_CHUNK
    nc = S // L
    f32 = jnp.float32

    def to_chunks(t):
        return jnp.moveaxis(t.astype(f32).reshape((Bsz, nc, L) + t.shape[2:]), 1, 0)

    xs = (to_chunks(xh), to_chunks(dt), to_chunks(Bm), to_chunks(Cm))
    causal = jnp.tril(jnp.ones((L, L), dtype=bool))[None, :, :, None, None]
    Af = A.astype(f32)

    def step(state, inp):
        xc, dtc, bc, cc = inp
        a_cum = jnp.cumsum(dtc * Af, axis=1)
        seg = a_cum[:, :, None] - a_cum[:, None, :]
        decay = jnp.exp(jnp.where(causal, seg, -jnp.inf))
        cb = jnp.einsum('blgn,bsgn->blsg', cc, bc)
        wts = cb[..., None] * decay * dtc[:, None]
        y_intra = jnp.einsum('blsgr,bsgrp->blgrp', wts, xc)
        y_inter = jnp.einsum('blgn,bgrpn->blgrp', cc, state) * jnp.exp(a_cum)[..., None]
        to_end = jnp.exp(a_cum[:, -1:] - a_cum) * dtc
        new_state = (state * jnp.exp(a_cum[:, -1])[..., None, None]
                     + jnp.einsum('bsgr,bsgrp,bsgn->bgrpn', to_end, xc, bc))
        return new_state, y_intra + y_inter

    state0 = jnp.zeros((Bsz, SSM_GROUPS, SSM_HEADS_PER_GROUP, SSM_HEADDIM, SSM_STATE), f32)
    _, ys = lax.scan(step, state0, xs)
    y = jnp.moveaxis(ys, 0, 1).reshape(xh.shape)
    return y.astype(xh.dtype)


def mamba2_mixer(h, w_in, conv_w, conv_b, dt_bias, a_log, d_skip, norm_g, w_out):
    Bsz, S, _ = h.shape
    G, R, P, N = SSM_GROUPS, SSM_HEADS_PER_GROUP, SSM_HEADDIM, SSM_STATE
    zxbcdt = h @ w_in
    z, xbc, dt = jnp.split(zxbcdt, [SSM_D_INNER, SSM_D_INNER + SSM_CONV_DIM], axis=-1)
    xbc = jax.nn.silu(causal_depthwise_conv(xbc, conv_w, conv_b))
    xs, Bm, Cm = jnp.split(xbc, [SSM_D_INNER, SSM_D_INNER + G * N], axis=-1)
    xh = xs.reshape(Bsz, S, G, R, P)
    Bm = Bm.reshape(Bsz, S, G, N)
    Cm = Cm.reshape(Bsz, S, G, N)
    dt = jax.nn.softplus((dt + dt_bias).astype(jnp.float32)).reshape(Bsz, S, G, R)
    A = -jnp.exp(a_log.astype(jnp.float32)).reshape(G, R)
    y = ssd_chunked_scan(xh, dt, A, Bm, Cm)
    y = y + xh * d_skip.reshape(G, R)[:, :, None]
    yg = (y.reshape(Bsz, S, G, R * P) * jax.nn.silu(z).reshape(Bsz, S, G, R * P))
    y = rms_norm(yg, norm_g).reshape(Bsz, S, SSM_D_INNER)
    return y @ w_out


def rel_position_bias(rel_table):
    l = jnp.arange(CHUNK)[:, None]
    j = jnp.arange(ATT_BAND)[None, :]
    dist = l + ATT_LEFT_CHUNKS * CHUNK - j
    bucket = jnp.clip(dist, -REL_MAX, REL_MAX) + REL_MAX
    return rel_table[:, bucket]


def chunked_rel_attention(q, k, v, rel_table):
    Bsz, S = q.shape[:2]
    nc = S // CHUNK
    pad = ATT_LEFT_CHUNKS * CHUNK
    kp = jnp.pad(k, ((0, 0), (pad, 0), (0, 0), (0, 0)))
    vp = jnp.pad(v, ((0, 0), (pad, 0), (0, 0), (0, 0)))
    qc = jnp.moveaxis(q.reshape(Bsz, nc, CHUNK, ATT_HEADS, ATT_HEADDIM), 1, 0)
    bias = rel_position_bias(rel_table).astype(jnp.float32)[None]
    scale = ATT_HEADDIM ** -0.5
    key_slot = jnp.arange(ATT_BAND)

    def one_chunk(args):
        i, qi = args
        start = i * CHUNK
        kb = lax.dynamic_slice_in_dim(kp, start, ATT_BAND, axis=1)
        vb = lax.dynamic_slice_in_dim(vp, start, ATT_BAND, axis=1)
        s = jnp.einsum('blhd,bjhd->bhlj', qi, kb).astype(jnp.float32) * scale + bias
        valid = key_slot >= pad - start
        s = jnp.where(valid, s, -jnp.inf)
        p = jax.nn.softmax(s, axis=-1)
        return jnp.einsum('bhlj,bjhd->blhd', p.astype(vb.dtype), vb)

    o = lax.map(one_chunk, (jnp.arange(nc), qc))
    return jnp.moveaxis(o, 0, 1).reshape(Bsz, S, ATT_D)


def chunked_attention_mixer(h, k, v, w_in, rel_table, w_out):
    Bsz, S, _ = h.shape
    q, z = jnp.split(h @ w_in, 2, axis=-1)
    o = chunked_rel_attention(q.reshape(Bsz, S, ATT_HEADS, ATT_HEADDIM), k, v, rel_table)
    return (o * jax.nn.silu(z)) @ w_out


def setup_inputs(seed: int = 0) -> dict:
    key = jax.random.key(seed)
    ks = jax.random.split(key, 20)
    f32 = jnp.float32

    def normal(k, shape, scale):
        return jax.random.normal(k, shape, f32) * scale

    x = normal(ks[0], (BATCH, SEQ, D_MODEL), 1.0)
    c = normal(ks[1], (BATCH, D_MODEL), 1.0)
    ada_w = normal(ks[2], (DEPTH, D_MODEL, 3 * D_MODEL), 0.5 * D_MODEL ** -0.5)
    ada_b = normal(ks[3], (DEPTH, 3 * D_MODEL), 0.01)
    pre_norm_g = 1.0 + normal(ks[4], (DEPTH, D_MODEL), 0.05)
    post_norm_g = 1.0 + normal(ks[5], (DEPTH, D_MODEL), 0.05)
    ssm_w_in = normal(ks[6], (N_A_LAYERS, D_MODEL, SSM_IN_DIM), D_MODEL ** -0.5)
    ssm_conv_w = normal(ks[7], (N_A_LAYERS, SSM_CONV, SSM_CONV_DIM), SSM_CONV ** -0.5)
    ssm_conv_b = normal(ks[8], (N_A_LAYERS, SSM_CONV_DIM), 0.01)
    dt0 = jnp.exp(jax.random.uniform(ks[9], (N_A_LAYERS, SSM_HEADS), f32,
                                     math.log(1e-3), math.log(1e-1)))
    ssm_dt_bias = dt0 + jnp.log(-jnp.expm1(-dt0))
    ssm_a_log = jnp.log(jax.random.uniform(ks[10], (N_A_LAYERS, SSM_HEADS), f32, 1.0, 16.0))
    ssm_d = 1.0 + normal(ks[11], (N_A_LAYERS, SSM_HEADS), 0.1)
    ssm_norm_g = 1.0 + normal(ks[12], (N_A_LAYERS, SSM_GROUPS, SSM_D_INNER // SSM_GROUPS), 0.05)
    ssm_w_out = normal(ks[13], (N_A_LAYERS, SSM_D_INNER, D_MODEL), SSM_D_INNER ** -0.5)
    kv_norm_g = 1.0 + normal(ks[14], (D_MODEL,), 0.05)
    w_kv = normal(ks[15], (D_MODEL, 2 * ATT_D), D_MODEL ** -0.5)
    att_w_in = normal(ks[16], (N_B_LAYERS, D_MODEL, 2 * ATT_D), D_MODEL ** -0.5)
    att_rel_bias = normal(ks[17], (N_B_LAYERS, ATT_HEADS, REL_BUCKETS), 0.5)
    att_w_out = normal(ks[18], (N_B_LAYERS, ATT_D, D_MODEL), ATT_D ** -0.5)
    return {"x": x, "c": c, "ada_w": ada_w, "ada_b": ada_b,
            "pre_norm_g": pre_norm_g, "post_norm_g": post_norm_g,
            "ssm_w_in": ssm_w_in, "ssm_conv_w": ssm_conv_w, "ssm_conv_b": ssm_conv_b,
            "ssm_dt_bias": ssm_dt_bias, "ssm_a_log": ssm_a_log, "ssm_d": ssm_d,
            "ssm_norm_g": ssm_norm_g, "ssm_w_out": ssm_w_out,
            "kv_norm_g": kv_norm_g, "w_kv": w_kv,
            "att_w_in": att_w_in, "att_rel_bias": att_rel_bias, "att_w_out": att_w_out}


def reference(x, c, ada_w, ada_b, pre_norm_g, post_norm_g, ssm_w_in, ssm_conv_w, ssm_conv_b,
              ssm_dt_bias, ssm_a_log, ssm_d, ssm_norm_g, ssm_w_out, kv_norm_g, w_kv,
              att_w_in, att_rel_bias, att_w_out):
    Bsz, S, _ = x.shape
    k = None
    v = None
    for layer in range(DEPTH):
        shift, scale, gate = ada_modulation(c, ada_w[layer], ada_b[layer])
        h = rms_norm(x, pre_norm_g[layer]) * (1.0 + scale) + shift
        if layer < N_A_LAYERS:
            i = layer
            y = mamba2_mixer(h, ssm_w_in[i], ssm_conv_w[i], ssm_conv_b[i], ssm_dt_bias[i],
                             ssm_a_log[i], ssm_d[i], ssm_norm_g[i], ssm_w_out[i])
        else:
            i = layer - N_A_LAYERS
            if i == 0:
                kv = rms_norm(x, kv_norm_g) @ w_kv
                k, v = jnp.split(kv, 2, axis=-1)
                k = k.reshape(Bsz, S, ATT_HEADS, ATT_HEADDIM)
                v = v.reshape(Bsz, S, ATT_HEADS, ATT_HEADDIM)
            y = chunked_attention_mixer(h, k, v, att_w_in[i], att_rel_bias[i], att_w_out[i])
        x = x + gate * rms_norm(y, post_norm_g[layer])
    return x
```

```python
import numpy as np
from contextlib import ExitStack
import concourse.bass as bass
import concourse.mybir as mybir
from concourse.bass_utils import run_bass_kernel_spmd

F32 = mybir.dt.float32
BF16 = mybir.dt.bfloat16
AF = mybir.ActivationFunctionType
ALU = mybir.AluOpType
AX = mybir.AxisListType

NEG = -30000.0


class Cfg:
    def __init__(self, D=4096, S=8192, NB=2, G=8, PTOK=2048):
        self.D, self.S, self.NB = D, S, NB
        self.PTOK = PTOK
        self.T = NB * S
        self.DI = 2 * D
        self.P = 64
        self.H = self.DI // 64
        self.G = G
        self.R = self.H // self.G
        self.N = 128
        self.GC = self.R * 64
        self.CD = self.DI + 2 * self.G * self.N
        self.IN = self.DI + self.CD + self.H
        self.AH = D // 128
        self.AD = D
        self.KC = D // 128
        self.EPS = 1e-6


class Ctx:
    def __init__(self, nc):
        self.nc = nc
        self.eng = {"pe": nc.tensor, "act": nc.scalar, "dve": nc.vector, "pool": nc.gpsimd, "sp": nc.sync}
        self.esem = {}
        self.cnt = {}
        self.seen = {e: {} for e in self.eng}
        for e in self.eng:
            self.esem[e] = nc.alloc_semaphore("es_" + e)
            self.cnt[e] = 0
        self.dsem = {}
        self.dcnt = {}
        self.state = {}
        self.pool = []
        self.nsem = 0

    def _st(self, key):
        s = self.state.get(key)
        if s is None:
            s = {"w": None, "r": {}}
            self.state[key] = s
        return s

    def _wait(self, e, ev):
        if ev is None:
            return
        sem, val, src = ev
        if src == "pe" and e == "pe":
            return
        sid = id(sem)
        if self.seen[e].get(sid, 0) >= val:
            return
        self.eng[e].wait_ge(sem, val)
        self.seen[e][sid] = val

    def _deps(self, e, reads, writes):
        for k in reads:
            self._wait(e, self._st(k)["w"])
        for k in writes:
            s = self._st(k)
            self._wait(e, s["w"])
            for ev in s["r"].values():
                self._wait(e, ev)

    def _commit(self, ev, reads, writes):
        for k in reads:
            s = self._st(k)
            old = s["r"].get(id(ev[0]))
            if old is None or old[1] < ev[1]:
                s["r"][id(ev[0])] = ev
        for k in writes:
            s = self._st(k)
            s["w"] = ev
            s["r"] = {}

    def op(self, e, fn, reads=(), writes=(), inc=True):
        self._deps(e, reads, writes)
        ins = fn(self.eng[e])
        if inc:
            self.cnt[e] += 1
            ins.then_inc(self.esem[e], 1)
            ev = (self.esem[e], self.cnt[e], e)
        else:
            ev = (self.esem[e], self.cnt[e] + 1, e)
        self._commit(ev, reads, writes)
        return ins

    def dma(self, q, out, in_, reads=(), writes=(), key=None, **kw):
        self._deps(q, reads, writes)
        if key is None:
            key = (tuple(writes) + tuple(reads))[0]
        if key not in self.dsem:
            if self.pool:
                self.dsem[key], self.dcnt[key] = self.pool.pop()
            else:
                self.nsem += 1
                self.dsem[key] = self.nc.alloc_semaphore("ds%d" % self.nsem)
                self.dcnt[key] = 0
        sem = self.dsem[key]
        self.dcnt[key] += 16
        self.eng[q].dma_start(out=out, in_=in_, **kw).then_inc(sem, 16)
        ev = (sem, self.dcnt[key], "dma")
        self._commit(ev, reads, writes)

    def barrier(self):
        evs = [(self.esem[e], self.cnt[e], "x") for e in self.eng if self.cnt[e] > 0]
        evs += [(self.dsem[k], self.dcnt[k], "dma") for k in self.dsem]
        for e in self.eng:
            for ev in evs:
                self._wait(e, ev)
        self.state = {}
        for k in list(self.dsem):
            self.pool.append((self.dsem[k], self.dcnt[k]))
        self.dsem = {}
        self.dcnt = {}


def _col_tiles(ncols, w):
    out = []
    c = 0
    while c < ncols:
        out.append((c, min(w, ncols - c)))
        c += w
    return out


def build(cfg, n_layers_a=2, n_layers_b=2, dbg=None):
    c = cfg
    D, T, S, KC = c.D, c.T, c.S, c.KC
    DI, CD, H, G, R, GC, IN = c.DI, c.CD, c.H, c.G, c.R, c.GC, c.IN
    AH, AD = c.AH, c.AD
    NL = n_layers_a + n_layers_b
    nc = bass.Bass("TRN2", target_bir_lowering=False)

    def din(name, shape, dt=F32):
        return nc.dram_tensor(name, list(shape), dt, kind="ExternalInput").ap()

    def dscr(name, shape, dt=F32):
        return nc.dram_tensor(name, list(shape), dt, kind=("ExternalOutput" if dbg else "Internal")).ap()

    x_in = din("x", [T, D])
    c_in = din("c", [c.NB, D])
    ada_w = din("ada_w", [4, D, 3 * D])
    ada_b = din("ada_b", [4, 3 * D])
    pre_g = din("pre_norm_g", [4, D])
    post_g = din("post_norm_g", [4, D])
    ssm_w_in = din("ssm_w_in", [2, D, IN])
    conv_w = din("ssm_conv_w", [2, 4, CD])
    conv_b = din("ssm_conv_b", [2, CD])
    dt_bias = din("ssm_dt_bias", [2, H])
    a_log = din("ssm_a_log", [2, H])
    ssm_d = din("ssm_d", [2, H])
    ssm_ng = din("ssm_norm_g", [2, G, GC])
    ssm_w_out = din("ssm_w_out", [2, DI, D])
    kv_g = din("kv_norm_g", [D])
    w_kv = din("w_kv", [D, 2 * AD])
    att_w_in = din("att_w_in", [2, D, 2 * AD])
    att_bm = din("att_bm", [2, AH, 128, 640])
    att_w_out = din("att_w_out", [2, AD, D])
    out = nc.dram_tensor("out", [T, D], F32, kind="ExternalOutput").ap()

    PTOK = c.PTOK

    class Pieces:
        def __init__(self, name, cols, dt=F32):
            self.t = [dscr("%s_p%d" % (name, i), [PTOK, cols], dt) for i in range(T // PTOK)]

        def rows(self, t0, n):
            p = t0 // PTOK
            o = t0 - p * PTOK
            assert o + n <= PTOK, (t0, n)
            return self.t[p][o:o + n]

    MODS = dscr("MODS", [4, 3, c.NB, D])
    U = Pieces("U", IN)
    XC = Pieces("XC", CD, BF16)
    DT = Pieces("DTs", H)
    YN = Pieces("YN", DI, BF16)
    Y2 = Pieces("Y2", D)
    KV = Pieces("KVs", 2 * AD, BF16)
    QS = Pieces("QS", AD, BF16)
    ZS = Pieces("ZS", AD)

    def wscr(name, K, ncols, w):
        return dscr(name, [len(_col_tiles(ncols, w)), 128, K // 128, w], BF16)

    W_IN = [wscr("W_IN%d" % i, D, IN, 512) for i in range(2)]
    W_OUT = [wscr("W_OUT%d" % i, DI, D, 256) for i in range(2)]
    W_KV = wscr("W_KV", D, 2 * AD, 512)
    W_AIN = [wscr("W_AIN%d" % i, D, 2 * AD, 512) for i in range(2)]
    W_AOUT = [wscr("W_AOUT%d" % i, AD, D, 512) for i in range(2)]

    cx = Ctx(nc)
    op, dma = cx.op, cx.dma
    uniq = [0]
    epsb = nc.alloc_sbuf_tensor("epsb", [128, 1], F32)

    def sbt(name, shape, dt):
        uniq[0] += 1
        return nc.sbuf_tensor("%s_%d" % (name, uniq[0]), shape, dt)

    def pst(name, shape, dt):
        uniq[0] += 1
        return nc.psum_tensor("%s_%d" % (name, uniq[0]), shape, dt)

    def phase_convert(src, K, ncols, w, dst):
        kcn = K // 128
        KB = min(8, kcn)
        with ExitStack() as es:
            cf = es.enter_context(sbt("cv_f", [128, 2, KB, w], F32))
            cb = es.enter_context(sbt("cv_b", [128, 2, KB, w], BF16))
            it = 0
            for ci, (c0, cw) in enumerate(_col_tiles(ncols, w)):
                for k0 in range(0, kcn, KB):
                    s = it % 2
                    srcap = src[k0 * 128:(k0 + KB) * 128, c0:c0 + cw].rearrange("(k p) n -> p k n", p=128)
                    dma("sp", cf[:, s, :, :cw], srcap, writes=[("cvf", s)])
                    e = ("act", "dve", "pool")[it % 3]
                    if e == "act":
                        op(e, lambda g: g.copy(cb[:, s, :, :cw], cf[:, s, :, :cw]), reads=[("cvf", s)], writes=[("cvb", s)])
                    else:
                        op(e, lambda g: g.tensor_copy(cb[:, s, :, :cw], cf[:, s, :, :cw]), reads=[("cvf", s)], writes=[("cvb", s)])
                    dma("sp", dst[ci, :, k0:k0 + KB, :cw], cb[:, s, :, :cw], reads=[("cvb", s)], key=("cvb_st", s))
                    it += 1
        cx.barrier()

    def phase_mod():
        with ExitStack() as es:
            cT = es.enter_context(sbt("m_ct", [128, c.NB, KC], F32))
            cs = es.enter_context(sbt("m_cs", [128, c.NB, KC], F32))
            wt = es.enter_context(sbt("m_w", [128, 2, 8, 512], F32))
            mod = es.enter_context(sbt("m_mod", [c.NB, 3 * D], F32))
            bt = es.enter_context(sbt("m_b", [c.NB, 3 * D], F32))
            gt = es.enter_context(sbt("m_g", [c.NB, 2, D], F32))
            ot = es.enter_context(sbt("m_o", [c.NB, 2, D], F32))
            ps = es.enter_context(pst("m_ps", [128, 2, 512], F32))
            with nc.allow_non_contiguous_dma(reason="tiny transposed load of c"):
                for b in range(c.NB):
                    dma("sp", cT[:, b, :], c_in[b].rearrange("(k p) -> p k", p=128), writes=["cT"], key=("cT", b))
            op("act", lambda g: g.activation(out=cs[:], in_=cT[:], func=AF.Silu), reads=["cT"], writes=["cs"])
            it = 0
            for l in range(NL):
                dma("sp", bt[:], ada_b[l].partition_broadcast(c.NB), writes=["mb"])
                dma("sp", gt[:, 0, :], pre_g[l].partition_broadcast(c.NB), writes=["mg0"])
                dma("sp", gt[:, 1, :], post_g[l].partition_broadcast(c.NB), writes=["mg1"])
                for ci, (c0, cw) in enumerate(_col_tiles(3 * D, 512)):
                    pb = ci % 2
                    for k0 in range(0, KC, 8):
                        kb = min(8, KC - k0)
                        s = it % 2
                        it += 1
                        dma("sp", wt[:, s, :kb, :], ada_w[l, k0 * 128:(k0 + kb) * 128, c0:c0 + cw].rearrange("(k p) n -> p k n", p=128),
                            writes=[("mw", s)])
                        for k in range(kb):
                            kk = k0 + k
                            op("pe", lambda g: g.matmul(ps[:c.NB, pb, :], cs[:, :, kk], wt[:, s, k, :], start=(kk == 0), stop=(kk == KC - 1)),
                               reads=["cs", ("mw", s)], writes=[("mps", pb)], inc=(k == kb - 1))
                    op("dve", lambda g: g.tensor_tensor(mod[:, c0:c0 + cw], ps[:c.NB, pb, :], bt[:, c0:c0 + cw], ALU.add),
                       reads=[("mps", pb), "mb"], writes=["mod"])
                op("dve", lambda g: g.scalar_tensor_tensor(ot[:, 0, :], mod[:, D:2 * D], 1.0, gt[:, 0, :], ALU.add, ALU.mult),
                   reads=["mod", "mg0"], writes=["mo0"])
                op("dve", lambda g: g.tensor_tensor(ot[:, 1, :], mod[:, 2 * D:3 * D], gt[:, 1, :], ALU.mult),
                   reads=["mod", "mg1"], writes=["mo1"])
                dma("sp", MODS[l, 0], ot[:, 0, :], reads=["mo0"], key="mst0")
                dma("sp", MODS[l, 1], mod[:, 0:D], reads=["mod"], key="mst1")
                dma("sp", MODS[l, 2], ot[:, 1, :], reads=["mo1"], key="mst2")
        cx.barrier()

    def phase_gemm(name, K, ncols, w, Wt, TG, src_fn, out_fn, src_tiles):
        kcn = K // 128
        cts = _col_tiles(ncols, w)
        TB = TG // 128
        with ExitStack() as es:
            hT = es.enter_context(sbt(name + "_hT", [128, kcn, TG], BF16))
            hb = es.enter_context(sbt(name + "_hb", [128, 2, K], BF16))
            wsb = es.enter_context(sbt(name + "_w", [128, 2, kcn, w], BF16))
            osb = es.enter_context(sbt(name + "_o", [128, 4, w], F32))
            ident = es.enter_context(sbt(name + "_id", [128, 128], BF16))
            identf = es.enter_context(sbt(name + "_idf", [128, 128], F32))
            pa = es.enter_context(pst(name + "_pa", [128, 4, 512], F32))
            pt = es.enter_context(pst(name + "_pt", [128, 2, 1024], BF16))
            make_ident(identf, ident)
            wit = 0
            oit = 0
            for tg in range(T // TG):
                for tb in range(TB):
                    slot = tb % 2
                    src_fn(tg * TB + tb, hb[:, slot, :], slot, src_tiles)
                    KT = min(8, kcn)
                    for k0 in range(0, kcn, KT):
                        pb = (k0 // KT) % 2
                        for k in range(KT):
                            op("pe", lambda g: g.transpose(pt[:, pb, k * 128:(k + 1) * 128], hb[:, slot, (k0 + k) * 128:(k0 + k + 1) * 128], ident[:]),
                               reads=[("hb", slot), "ident"], writes=[("pt", pb)], inc=(k == KT - 1))
                        dst = hT[:, k0:k0 + KT, tb * 128:(tb + 1) * 128]
                        srcp = pt[:, pb, :KT * 128].rearrange("p (k t) -> p k t", k=KT)
                        if pb == 0:
                            op("act", lambda g: g.copy(dst, srcp), reads=[("pt", pb)], writes=[("hT", tb)])
                        else:
                            op("dve", lambda g: g.tensor_copy(dst, srcp), reads=[("pt", pb)], writes=[("hT", tb)])
                for ci, (c0, cw) in enumerate(cts):
                    ws = wit % 2
                    wit += 1
                    dma("sp", wsb[:, ws, :, :cw], Wt[ci, :, :, :cw], writes=[("w", ws)])
                    for tb in range(TB):
                        pb = oit % 4
                        oit += 1
                        for k in range(kcn):
                            op("pe", lambda g: g.matmul(pa[:, pb, :cw], hT[:, k, tb * 128:(tb + 1) * 128], wsb[:, ws, k, :cw],
                                                        start=(k == 0), stop=(k == kcn - 1)),
                               reads=[("hT", tb), ("w", ws)], writes=[("pa", pb)], inc=(k == kcn - 1))
                        out_fn(tg * TB + tb, c0, cw, pa[:, pb, :cw], ("pa", pb), osb[:, pb, :cw], ("osb", pb))
        cx.barrier()

    def make_ident(identf, ident):
        op("pool", lambda g: g.memset(identf[:], 1.0), writes=["identf"])
        op("pool", lambda g: g.affine_select(identf[:], identf[:], [[-1, 128]], ALU.is_equal, 0.0, base=0, channel_multiplier=1),
           reads=["identf"], writes=["identf"])
        op("pool", lambda g: g.tensor_copy(ident[:], identf[:]), reads=["identf"], writes=["ident"])

    def norm_src_factory(xsrc, layer, use_mod):
        def alloc():
            return (sbt("ns_x", [128, 1, D], F32), sbt("ns_g", [128, D], F32),
                    sbt("ns_s", [128, D], F32), sbt("ns_ss", [128, 1, 2], F32),
                    sbt("ns_h", [128, 2], F32))
        state = {"b": -1, "i": 0}

        def fn(tbg, dst, slot, tiles):
            xt, gm, sh, ss, hf = tiles
            b = (tbg * 128) // S
            if b != state["b"]:
                state["b"] = b
                if use_mod:
                    dma("sp", gm[:], MODS[layer, 0, b].partition_broadcast(128), writes=["gm"])
                    dma("sp", sh[:], MODS[layer, 1, b].partition_broadcast(128), writes=["sh"])
                else:
                    dma("sp", gm[:], kv_g.partition_broadcast(128), writes=["gm"])
            xs = 0
            dma("sp", xt[:, xs, :], xsrc[tbg * 128:(tbg + 1) * 128, :], writes=[("nx", xs)])
            op("dve", lambda g: g.memset(ss[:, xs, :], 0.0), writes=[("nss", xs)])
            op("act", lambda g: g.activation(out=dst, in_=xt[:, xs, :], func=AF.Square, accum_out=ss[:, xs, 0:1]),
               reads=[("nx", xs)], writes=[("hb", slot), ("nss", xs)])
            op("act", lambda g: g.activation(out=ss[:, xs, 1:2], in_=ss[:, xs, 0:1], func=AF.Ln, bias=epsb[:], scale=1.0 / D),
               reads=[("nss", xs), "epsb"], writes=[("nss", xs)])
            op("act", lambda g: g.activation(out=ss[:, xs, 1:2], in_=ss[:, xs, 1:2], func=AF.Exp, scale=-0.5),
               reads=[("nss", xs)], writes=[("nss", xs)])
            if use_mod:
                op("dve", lambda g: g.scalar_tensor_tensor(xt[:, xs, :], xt[:, xs, :], ss[:, xs, 1:2], gm[:], ALU.mult, ALU.mult),
                   reads=[("nx", xs), ("nss", xs), "gm"], writes=[("nx", xs)])
                op("pool", lambda g: g.tensor_tensor(dst, xt[:, xs, :], sh[:], ALU.add), reads=[("nx", xs), "sh"], writes=[("hb", slot)])
            else:
                op("dve", lambda g: g.scalar_tensor_tensor(dst, xt[:, xs, :], ss[:, xs, 1:2], gm[:], ALU.mult, ALU.mult),
                   reads=[("nx", xs), ("nss", xs), "gm"], writes=[("hb", slot)])
        return alloc, fn

    def load_src_factory(src, ncol):
        def alloc():
            return ()

        def fn(tbg, dst, slot, tiles):
            dma("sp", dst, src.rows(tbg * 128, 128)[:, 0:ncol], writes=[("hb", slot)])
        return alloc, fn

    def run_gemm(name, K, ncols, w, Wt, TG, src_factory, out_fn):
        alloc, fn = src_factory
        tl = alloc()
        if len(tl) == 0:
            phase_gemm(name, K, ncols, w, Wt, TG, fn, out_fn, ())
        else:
            with ExitStack() as es2:
                aa = tuple(es2.enter_context(t) for t in tl)
                phase_gemm(name, K, ncols, w, Wt, TG, fn, out_fn, aa)

    def out_store_f32(dst):
        def fn(tbg, c0, cw, ps, pskey, osb, okey):
            e = "act" if (tbg + c0 // 128) % 2 == 0 else "dve"
            if e == "act":
                op("act", lambda g: g.copy(osb, ps), reads=[pskey], writes=[okey])
            else:
                op("dve", lambda g: g.tensor_copy(osb, ps), reads=[pskey], writes=[okey])
            dma("pool", dst.rows(tbg * 128, 128)[:, c0:c0 + cw], osb, reads=[okey], key=("ost", okey))
        return fn

    def out_store_split(dst_b, nb_cols, dst_f):
        def fn(tbg, c0, cw, ps, pskey, osb, okey):
            if c0 < nb_cols:
                ob = osb.bitcast(BF16)[:, :cw]
                op("act", lambda g: g.copy(ob, ps), reads=[pskey], writes=[okey])
                dma("pool", dst_b.rows(tbg * 128, 128)[:, c0:c0 + cw], ob, reads=[okey], key=("ost", okey))
            else:
                op("dve", lambda g: g.tensor_copy(osb, ps), reads=[pskey], writes=[okey])
                dma("pool", dst_f.rows(tbg * 128, 128)[:, c0 - nb_cols:c0 - nb_cols + cw], osb, reads=[okey], key=("ost", okey))
        return fn

    def phase_conv(i):
        CW = 512
        with ExitStack() as es:
            cw_t = es.enter_context(sbt("cv_w", [128, 4, CW], F32))
            cb_t = es.enter_context(sbt("cv_bi", [128, CW], F32))
            ut = es.enter_context(sbt("cv_u", [128, 2, 4, CW], F32))
            mt = es.enter_context(sbt("cv_m", [128, 4, CW], F32))
            ot = es.enter_context(sbt("cv_o", [128, 2, CW], BF16))
            it = 0
            for c0, cw in _col_tiles(CD, CW):
                for k in range(4):
                    dma("sp", cw_t[:, k, :cw], conv_w[i, k, c0:c0 + cw].partition_broadcast(128), writes=[("cw", k)])
                dma("sp", cb_t[:, :cw], conv_b[i, c0:c0 + cw].partition_broadcast(128), writes=["cb"])
                for tb in range(T // 128):
                    s = it % 2
                    it += 1
                    t0 = tb * 128
                    first = (t0 % S == 0)
                    for k in range(4):
                        sh = 3 - k
                        if first and sh > 0:
                            op("pool", lambda g: g.memset(ut[:, s, k, :cw], 0.0), writes=[("u", s, k)])
                            dma("sp", ut[sh:128, s, k, :cw], U.rows(t0, 128 - sh)[:, DI + c0:DI + c0 + cw], writes=[("u", s, k)])
                        elif sh > 0 and t0 % PTOK == 0:
                            dma("sp", ut[0:sh, s, k, :cw], U.rows(t0 - sh, sh)[:, DI + c0:DI + c0 + cw], writes=[("u", s, k)])
                            dma("sp", ut[sh:128, s, k, :cw], U.rows(t0, 128 - sh)[:, DI + c0:DI + c0 + cw], writes=[("u", s, k)])
                        else:
                            dma("sp", ut[:, s, k, :cw], U.rows(t0 - sh, 128)[:, DI + c0:DI + c0 + cw], writes=[("u", s, k)])
                    for k in range(4):
                        e = "dve" if k % 2 == 0 else "pool"
                        op(e, lambda g: g.tensor_tensor(mt[:, k, :cw], ut[:, s, k, :cw], cw_t[:, k, :cw], ALU.mult),
                           reads=[("u", s, k), ("cw", k)], writes=[("m", k)])
                    op("dve", lambda g: g.tensor_tensor(mt[:, 0, :cw], mt[:, 0, :cw], mt[:, 2, :cw], ALU.add),
                       reads=[("m", 0), ("m", 2)], writes=[("m", 0)])
                    op("pool", lambda g: g.tensor_tensor(mt[:, 1, :cw], mt[:, 1, :cw], mt[:, 3, :cw], ALU.add),
                       reads=[("m", 1), ("m", 3)], writes=[("m", 1)])
                    op("pool", lambda g: g.tensor_tensor(mt[:, 1, :cw], mt[:, 1, :cw], cb_t[:, :cw], ALU.add),
                       reads=[("m", 1), "cb"], writes=[("m", 1)])
                    op("dve", lambda g: g.tensor_tensor(mt[:, 0, :cw], mt[:, 0, :cw], mt[:, 1, :cw], ALU.add),
                       reads=[("m", 0), ("m", 1)], writes=[("m", 0)])
                    op("act", lambda g: g.activation(out=ot[:, s, :cw], in_=mt[:, 0, :cw], func=AF.Silu),
                       reads=[("m", 0)], writes=[("co", s)])
                    dma("pool", XC.rows(t0, 128)[:, c0:c0 + cw], ot[:, s, :cw], reads=[("co", s)], key=("cost", s))
        cx.barrier()
        with ExitStack() as es:
            bt = es.enter_context(sbt("dt_b", [128, H], F32))
            ut = es.enter_context(sbt("dt_u", [128, 2, H], F32))
            at = es.enter_context(sbt("dt_a", [128, 2, H], F32))
            ot = es.enter_context(sbt("dt_o", [128, 2, H], F32))
            dma("sp", bt[:], dt_bias[i].partition_broadcast(128), writes=["dtb"])
            for tb in range(T // 128):
                s = tb % 2
                t0 = tb * 128
                dma("sp", ut[:, s, :], U.rows(t0, 128)[:, DI + CD:DI + CD + H], writes=[("dtu", s)])
                op("dve", lambda g: g.tensor_tensor(ut[:, s, :], ut[:, s, :], bt[:], ALU.add), reads=[("dtu", s), "dtb"], writes=[("dtu", s)])
                op("dve", lambda g: g.tensor_scalar(at[:, s, :], ut[:, s, :], -1.0, None, ALU.mult), reads=[("dtu", s)], writes=[("dta", s)])
                op("dve", lambda g: g.tensor_tensor(at[:, s, :], at[:, s, :], ut[:, s, :], ALU.min), reads=[("dtu", s), ("dta", s)], writes=[("dta", s)])
                op("act", lambda g: g.activation(out=at[:, s, :], in_=at[:, s, :], func=AF.Exp), reads=[("dta", s)], writes=[("dta", s)])
                op("act", lambda g: g.activation(out=at[:, s, :], in_=at[:, s, :], func=AF.Ln, bias=1.0), reads=[("dta", s)], writes=[("dta", s)])
                op("dve", lambda g: g.scalar_tensor_tensor(ot[:, s, :], ut[:, s, :], 0.0, at[:, s, :], ALU.max, ALU.add),
                   reads=[("dtu", s), ("dta", s)], writes=[("dto", s)])
                dma("pool", DT.rows(t0, 128), ot[:, s, :], reads=[("dto", s)], key=("dtst", s))
        cx.barrier()

    def phase_ssd(i):
        L = 64
        RL = R * L
        NH2 = (GC + 511) // 512
        with ExitStack() as es:
            xs = es.enter_context(sbt("s_xs", [L, 2, GC], BF16))
            bc = es.enter_context(sbt("s_bc", [L, 2, 256], BF16))
            dtt = es.enter_context(sbt("s_dt", [L, 2, R], F32))
            zt = es.enter_context(sbt("s_z", [L, 2, GC], F32))
            arep = es.enter_context(sbt("s_A", [128, H], F32))
            drep = es.enter_context(sbt("s_D", [128, H], F32))
            ngt = es.enter_context(sbt("s_ng", [128, GC], F32))
            tri = es.enter_context(sbt("s_tri", [L, L], F32))
            u01 = es.enter_context(sbt("s_u01", [L, L], F32))
            ones = es.enter_context(sbt("s_one", [L, 128], F32))
            identf = es.enter_context(sbt("s_idf", [128, 128], F32))
            ident = es.enter_context(sbt("s_id", [128, 128], BF16))
            dtA = es.enter_context(sbt("s_dtA", [L, R], F32))
            Rm = es.enter_context(sbt("s_Rm", [L, R, L], F32))
            Et = es.enter_context(sbt("s_E", [L, R, L], BF16))
            BCT = es.enter_context(sbt("s_BT", [128, 2, L], BF16))
            cbm = es.enter_context(sbt("s_cbm", [L, L], F32))
            Wt_ = es.enter_context(sbt("s_W", [L, R, L], BF16))
            xdt = es.enter_context(sbt("s_xdt", [L, R, 64], BF16))
            xe = es.enter_context(sbt("s_xe", [L, R, 64], BF16))
            ex = es.enter_context(sbt("s_ex", [128, 3, R], F32))
            ysb = es.enter_context(sbt("s_y", [L, GC], F32))
            t1 = es.enter_context(sbt("s_t1", [L, GC], F32))
            t2 = es.enter_context(sbt("s_t2", [L, GC], F32))
            sz = es.enter_context(sbt("s_sz", [L, GC], F32))
            ss = es.enter_context(sbt("s_ss", [L, 2], F32))
            yn = es.enter_context(sbt("s_yn", [L, 2, GC], BF16))
            ST = es.enter_context(sbt("s_ST", [128, GC], F32))
            STb = es.enter_context(sbt("s_STb", [128, GC], BF16))
            p01 = es.enter_context(pst("s_p01", [128, 2, 512], F32))
            p23 = es.enter_context(pst("s_p23", [128, 2, 512], F32))
            p45 = es.enter_context(pst("s_p45", [128, 2, 512], F32))
            p6 = es.enter_context(pst("s_p6", [128, 1024], BF16))
            p7 = es.enter_context(pst("s_p7", [128, 512], F32))
            make_ident(identf, ident)
            op("pool", lambda g: g.memset(tri[:], 1.0), writes=["tri"])
            op("pool", lambda g: g.affine_select(tri[:], tri[:], [[1, L]], ALU.is_ge, 0.0, base=0, channel_multiplier=-1),
               reads=["tri"], writes=["tri"])
            op("pool", lambda g: g.memset(u01[:], 1.0), writes=["u01"])
            op("pool", lambda g: g.affine_select(u01[:], u01[:], [[-1, L]], ALU.is_gt, 0.0, base=0, channel_multiplier=1),
               reads=["u01"], writes=["u01"])
            op("pool", lambda g: g.memset(ones[:], 1.0), writes=["ones"])
            dma("sp", arep[:], a_log[i].partition_broadcast(128), writes=["arep"])
            op("act", lambda g: g.activation(out=arep[:], in_=arep[:], func=AF.Exp), reads=["arep"], writes=["arep"])
            op("dve", lambda g: g.tensor_scalar(arep[:], arep[:], -1.0, None, ALU.mult), reads=["arep"], writes=["arep"])
            dma("sp", drep[:], ssm_d[i].partition_broadcast(128), writes=["drep"])
            yit = 0
            for b in range(c.NB):
                for gI in range(G):
                    dma("sp", ngt[:], ssm_ng[i, gI].partition_broadcast(128), writes=["ngt"])
                    op("dve", lambda g: g.memset(ST[:], 0.0), writes=["ST"])
                    op("pool", lambda g: g.memset(STb[:], 0.0), writes=["STb"])
                    for ch in range(S // L):
                        s = ch % 2
                        t0 = b * S + ch * L
                        dma("sp", xs[:, s, :], XC.rows(t0, L)[:, gI * GC:(gI + 1) * GC], writes=[("xs", s)])
                        dma("sp", bc[:, s, 0:128], XC.rows(t0, L)[:, DI + gI * 128:DI + (gI + 1) * 128], writes=[("bcB", s)])
                        dma("sp", bc[:, s, 128:256], XC.rows(t0, L)[:, DI + G * 128 + gI * 128:DI + G * 128 + (gI + 1) * 128], writes=[("bcC", s)])
                        dma("sp", dtt[:, s, :], DT.rows(t0, L)[:, gI * R:(gI + 1) * R], writes=[("dt", s)])
                        dma("sp", zt[:, s, :], U.rows(t0, L)[:, gI * GC:(gI + 1) * GC], writes=[("z", s)])
                        xs3 = xs[:, s, :].rearrange("p (r q) -> p r q", r=R)
                        op("dve", lambda g: g.tensor_tensor(dtA[:], dtt[:, s, :], arep[:L, gI * R:(gI + 1) * R], ALU.mult),
                           reads=[("dt", s), "arep"], writes=["dtA"])
                        op("dve", lambda g: g.tensor_tensor(Rm[:], tri[:].unsqueeze(1).to_broadcast([L, R, L]),
                                                            dtA[:].unsqueeze(2).to_broadcast([L, R, L]), ALU.mult),
                           reads=["tri", "dtA"], writes=["Rm"])
                        Rm2 = Rm[:].rearrange("p r l -> p (r l)")
                        for hh in range((RL + 511) // 512):
                            n0, n1 = hh * 512, min(RL, (hh + 1) * 512)
                            op("pe", lambda g: g.matmul(p01[:L, hh, :n1 - n0], u01[:], Rm2[:, n0:n1], start=True, stop=True),
                               reads=["u01", "Rm"], writes=[("p01", hh)])
                        op("pe", lambda g: g.matmul(p7[:L, 0:R], tri[:], dtA[:], start=True, stop=True), reads=["tri", "dtA"], writes=["p7"], inc=False)
                        op("pe", lambda g: g.matmul(p7[:L, 64:64 + R], u01[:], dtA[:], start=True, stop=True), reads=["u01", "dtA"], writes=["p7"], inc=False)
                        op("pe", lambda g: g.matmul(p7[:, 128:128 + R], ones[:], dtA[:], start=True, stop=True), reads=["ones", "dtA"], writes=["p7"])
                        op("act", lambda g: g.activation(out=ex[:L, 0, :], in_=p7[:L, 0:R], func=AF.Exp), reads=["p7"], writes=["ex0"])
                        op("act", lambda g: g.activation(out=ex[:L, 1, :], in_=p7[:L, 64:64 + R], func=AF.Exp), reads=["p7"], writes=["ex1"])
                        op("act", lambda g: g.activation(out=ex[:, 2, :], in_=p7[:, 128:128 + R], func=AF.Exp), reads=["p7"], writes=["ex2"])
                        for hh in range((RL + 511) // 512):
                            n0, n1 = hh * 512, min(RL, (hh + 1) * 512)
                            op("act", lambda g: g.activation(out=Et[:].rearrange("p r l -> p (r l)")[:, n0:n1], in_=p01[:L, hh, :n1 - n0], func=AF.Exp),
                               reads=[("p01", hh)], writes=["E"])
                        op("pe", lambda g: g.transpose(p6[:, 0:L], bc[:, s, 0:128], ident[:L, :L]), reads=[("bcB", s), "ident"], writes=["p6a"])
                        op("pe", lambda g: g.transpose(p6[:, L:2 * L], bc[:, s, 128:256], ident[:L, :L]), reads=[("bcC", s), "ident"], writes=["p6b"])
                        op("act", lambda g: g.copy(BCT[:, 0, :], p6[:, 0:L]), reads=["p6a"], writes=["BT"])
                        op("dve", lambda g: g.tensor_copy(BCT[:, 1, :], p6[:, L:2 * L]), reads=["p6b"], writes=["CT"])
                        op("pe", lambda g: g.matmul(p7[:L, 256:256 + L], BCT[:, 0, :], BCT[:, 1, :], start=True, stop=True),
                           reads=["BT", "CT"], writes=["p7c"])
                        op("dve", lambda g: g.tensor_tensor(cbm[:], p7[:L, 256:256 + L], tri[:], ALU.mult), reads=["p7c", "tri"], writes=["cbm"])
                        op("dve", lambda g: g.tensor_tensor(Wt_[:], Et[:], cbm[:].unsqueeze(1).to_broadcast([L, R, L]), ALU.mult),
                           reads=["E", "cbm"], writes=["W"])
                        op("pool", lambda g: g.tensor_tensor(xdt[:], xs3, dtt[:, s, :].unsqueeze(2).to_broadcast([L, R, 64]), ALU.mult),
                           reads=[("xs", s), ("dt", s)], writes=["xdt"])
                        for r in range(R):
                            hh, off = (r * 64) // 512, (r * 64) % 512
                            op("pe", lambda g: g.matmul(p23[:L, hh, off:off + 64], Wt_[:, r, :], xdt[:, r, :], start=True, stop=True),
                               reads=["W", "xdt"], writes=[("p23", hh)], inc=(r == R - 1 or (r * 64 + 64) % 512 == 0))
                        for hh in range(NH2):
                            n0, n1 = hh * 512, min(GC, (hh + 1) * 512)
                            op("pe", lambda g: g.matmul(p45[:L, hh, :n1 - n0], BCT[:, 1, :], STb[:, n0:n1], start=True, stop=True),
                               reads=["CT", "STb"], writes=[("p45", hh)])
                        for hh in range(NH2):
                            n0, n1 = hh * 512, min(GC, (hh + 1) * 512)
                            r0, r1 = n0 // 64, n1 // 64
                            op("act", lambda g: g.copy(ysb[:, n0:n1], p23[:L, hh, :n1 - n0]), reads=[("p23", hh)], writes=[("ysb", hh)])
                            op("dve", lambda g: g.tensor_tensor(t1[:, n0:n1].rearrange("p (r q) -> p r q", q=64),
                                                                p45[:L, hh, :n1 - n0].rearrange("p (r q) -> p r q", q=64),
                                                                ex[:L, 0, r0:r1].unsqueeze(2).to_broadcast([L, r1 - r0, 64]), ALU.mult),
                               reads=[("p45", hh), "ex0"], writes=[("t1", hh)])
                            op("dve", lambda g: g.tensor_tensor(t1[:, n0:n1], t1[:, n0:n1], ysb[:, n0:n1], ALU.add),
                               reads=[("t1", hh), ("ysb", hh)], writes=[("t1", hh)])
                        op("pool", lambda g: g.tensor_tensor(t2[:].rearrange("p (r q) -> p r q", q=64), xs3,
                                                             drep[:L, gI * R:(gI + 1) * R].unsqueeze(2).to_broadcast([L, R, 64]), ALU.mult),
                           reads=[("xs", s), "drep"], writes=["t2"])
                        t1keys = [("t1", hh) for hh in range(NH2)]
                        op("pool", lambda g: g.tensor_tensor(t2[:], t2[:], t1[:], ALU.add), reads=["t2"] + t1keys, writes=["t2"])
                        op("act", lambda g: g.activation(out=sz[:], in_=zt[:, s, :], func=AF.Silu), reads=[("z", s)], writes=["sz"])
                        op("dve", lambda g: g.tensor_tensor(t2[:], t2[:], sz[:], ALU.mult), reads=["t2", "sz"], writes=["t2"])
                        op("dve", lambda g: g.memset(ss[:], 0.0), writes=["ss"])
                        op("act", lambda g: g.activation(out=sz[:], in_=t2[:], func=AF.Square, accum_out=ss[:, 0:1]),
                           reads=["t2"], writes=["sz", "ss"])
                        op("act", lambda g: g.activation(out=ss[:, 1:2], in_=ss[:, 0:1], func=AF.Ln, bias=epsb[:L, :], scale=1.0 / GC), reads=["ss", "epsb"], writes=["ss"])
                        op("act", lambda g: g.activation(out=ss[:, 1:2], in_=ss[:, 1:2], func=AF.Exp, scale=-0.5), reads=["ss"], writes=["ss"])
                        ys = yit % 2
                        yit += 1
                        op("dve", lambda g: g.scalar_tensor_tensor(yn[:, ys, :], t2[:], ss[:, 1:2], ngt[:L, :], ALU.mult, ALU.mult),
                           reads=["t2", "ss", "ngt"], writes=[("yn", ys)])
                        dma("pool", YN.rows(t0, L)[:, gI * GC:(gI + 1) * GC], yn[:, ys, :], reads=[("yn", ys)], key=("ynst", ys))
                        op("pool", lambda g: g.tensor_tensor(xe[:], xdt[:], ex[:L, 1, :].unsqueeze(2).to_broadcast([L, R, 64]), ALU.mult),
                           reads=["xdt", "ex1"], writes=["xe"])
                        xe2 = xe[:].rearrange("p r q -> p (r q)")
                        for hh in range(NH2):
                            n0, n1 = hh * 512, min(GC, (hh + 1) * 512)
                            op("pe", lambda g: g.matmul(p01[:, hh, :n1 - n0], bc[:, s, 0:128], xe2[:, n0:n1], start=True, stop=True),
                               reads=[("bcB", s), "xe"], writes=[("p01", hh)])
                        op("dve", lambda g: g.tensor_tensor(ST[:].rearrange("p (r q) -> p r q", q=64), ST[:].rearrange("p (r q) -> p r q", q=64),
                                                            ex[:, 2, :].unsqueeze(2).to_broadcast([128, R, 64]), ALU.mult),
                           reads=["ST", "ex2"], writes=["ST"])
                        for hh in range(NH2):
                            n0, n1 = hh * 512, min(GC, (hh + 1) * 512)
                            op("dve", lambda g: g.tensor_tensor(ST[:, n0:n1], ST[:, n0:n1], p01[:, hh, :n1 - n0], ALU.add),
                               reads=["ST", ("p01", hh)], writes=["ST"])
                        op("act", lambda g: g.copy(STb[:], ST[:]), reads=["ST"], writes=["STb"])
        cx.barrier()

    def phase_post(layer, xsrc):
        with ExitStack() as es:
            xt = es.enter_context(sbt("pn_x", [128, 2, D], F32))
            yt = es.enter_context(sbt("pn_y", [128, 2, D], F32))
            gp = es.enter_context(sbt("pn_g", [128, D], F32))
            junk = es.enter_context(sbt("pn_j", [128, D], F32))
            ss = es.enter_context(sbt("pn_ss", [128, 2, 2], F32))
            ot = es.enter_context(sbt("pn_o", [128, 2, D], F32))
            for tb in range(T // 128):
                s = tb % 2
                t0 = tb * 128
                if t0 % S == 0:
                    dma("sp", gp[:], MODS[layer, 2, t0 // S].partition_broadcast(128), writes=["gp"])
                dma("sp", xt[:, s, :], xsrc[t0:t0 + 128, :], writes=[("px", s)])
                dma("sp", yt[:, s, :], Y2.rows(t0, 128), writes=[("py", s)])
                op("dve", lambda g: g.memset(ss[:, s, :], 0.0), writes=[("pss", s)])
                op("act", lambda g: g.activation(out=junk[:], in_=yt[:, s, :], func=AF.Square, accum_out=ss[:, s, 0:1]),
                   reads=[("py", s)], writes=["junk", ("pss", s)])
                op("act", lambda g: g.activation(out=ss[:, s, 1:2], in_=ss[:, s, 0:1], func=AF.Ln, bias=epsb[:], scale=1.0 / D), reads=[("pss", s), "epsb"], writes=[("pss", s)])
                op("act", lambda g: g.activation(out=ss[:, s, 1:2], in_=ss[:, s, 1:2], func=AF.Exp, scale=-0.5), reads=[("pss", s)], writes=[("pss", s)])
                op("dve", lambda g: g.scalar_tensor_tensor(yt[:, s, :], yt[:, s, :], ss[:, s, 1:2], gp[:], ALU.mult, ALU.mult),
                   reads=[("py", s), ("pss", s), "gp"], writes=[("py", s)])
                op("pool", lambda g: g.tensor_tensor(ot[:, s, :], yt[:, s, :], xt[:, s, :], ALU.add), reads=[("py", s), ("px", s)], writes=[("po", s)])
                dma("pool", out[t0:t0 + 128, :], ot[:, s, :], reads=[("po", s)], key=("post", s))
        cx.barrier()

    def phase_attn(i):
        NQ = S // 128
        scale = 128 ** -0.5
        with ExitStack() as es:
            qt = es.enter_context(sbt("a_q", [128, NQ, 128], BF16))
            kt = es.enter_context(sbt("a_k", [128, NQ, 128], BF16))
            vt = es.enter_context(sbt("a_v", [128, NQ, 128], BF16))
            zt = es.enter_context(sbt("a_z", [128, NQ, 128], F32))
            qT = es.enter_context(sbt("a_qT", [128, NQ * 128], BF16))
            kT = es.enter_context(sbt("a_kT", [128, NQ * 128], BF16))
            og = es.enter_context(sbt("a_og", [128, NQ, 128], BF16))
            bm = es.enter_context(sbt("a_bm", [128, 640], F32))
            mk = es.enter_context(sbt("a_mk", [128, 640], F32))
            st = es.enter_context(sbt("a_s", [128, 2, 640], F32))
            pt_ = es.enter_context(sbt("a_p", [128, 2, 640], BF16))
            pT = es.enter_context(sbt("a_pT", [128, 2, 5, 128], BF16))
            sm = es.enter_context(sbt("a_st", [128, 2, 4], F32))
            ot = es.enter_context(sbt("a_o", [128, 2, 128], F32))
            ident = es.enter_context(sbt("a_id", [128, 128], BF16))
            identf = es.enter_context(sbt("a_idf", [128, 128], F32))
            pss = es.enter_context(pst("a_ps", [128, 2, 1024], F32))
            ptp = es.enter_context(pst("a_pt", [128, 2, 1024], BF16))
            po = es.enter_context(pst("a_po", [128, 2, 512], F32))
            make_ident(identf, ident)
            op("pool", lambda g: g.memset(mk[:], 0.0), writes=["mk"])
            op("pool", lambda g: g.memset(mk[0:64, 576:640], NEG), reads=[], writes=["mk"])
            op("pool", lambda g: g.memset(mk[64:128, 0:64], NEG), reads=[], writes=["mk"])
            it = 0
            for b in range(c.NB):
                for h in range(AH):
                    tb0 = b * S
                    PS_ = min(PTOK, S)
                    for pi in range(S // PS_):
                        tp = tb0 + pi * PS_
                        nsl = slice(pi * (PS_ // 128), (pi + 1) * (PS_ // 128))
                        dma("sp", qt[:, nsl, :], QS.rows(tp, PS_)[:, h * 128:(h + 1) * 128].rearrange("(n p) d -> p n d", p=128), writes=["q"], key="q")
                        dma("sp", kt[:, nsl, :], KV.rows(tp, PS_)[:, h * 128:(h + 1) * 128].rearrange("(n p) d -> p n d", p=128), writes=["k"], key="k")
                        dma("sp", vt[:, nsl, :], KV.rows(tp, PS_)[:, AD + h * 128:AD + (h + 1) * 128].rearrange("(n p) d -> p n d", p=128), writes=["v"], key="v")
                        dma("sp", zt[:, nsl, :], ZS.rows(tp, PS_)[:, h * 128:(h + 1) * 128].rearrange("(n p) d -> p n d", p=128), writes=["z"], key="z")
                    dma("sp", bm[:], att_bm[i, h], writes=["bm"])
                    op("pool", lambda g: g.tensor_tensor(bm[:], bm[:], mk[:], ALU.add), reads=["bm", "mk"], writes=["bm"])
                    op("act", lambda g: g.activation(out=zt[:], in_=zt[:], func=AF.Silu), reads=["z"], writes=["z"])
                    for src, dstT, nm in ((qt, qT, "qT"), (kt, kT, "kT")):
                        for n0 in range(0, NQ, 8):
                            nb = min(8, NQ - n0)
                            pb = (n0 // 8) % 2
                            for n in range(nb):
                                op("pe", lambda g: g.transpose(ptp[:, pb, n * 128:(n + 1) * 128], src[:, n0 + n, :], ident[:]),
                                   reads=["q" if nm == "qT" else "k", "ident"], writes=[("ptp", pb)], inc=(n == nb - 1))
                            if pb == 0:
                                op("act", lambda g: g.copy(dstT[:, n0 * 128:(n0 + nb) * 128], ptp[:, pb, :nb * 128]), reads=[("ptp", pb)], writes=[nm])
                            else:
                                op("dve", lambda g: g.tensor_copy(dstT[:, n0 * 128:(n0 + nb) * 128], ptp[:, pb, :nb * 128]), reads=[("ptp", pb)], writes=[nm])
                    for qi in range(NQ):
                        s = it % 2
                        it += 1
                        kb0 = max(0, qi - 4)
                        nkb = qi - kb0 + 1
                        j0 = (kb0 - (qi - 4)) * 128
                        nk = nkb * 128
                        for (a0, a1) in ((0, min(nk, 512)), (512, nk)):
                            if a1 <= a0:
                                continue
                            op("pe", lambda g: g.matmul(pss[:, s, a0:a1], qT[:, qi * 128:(qi + 1) * 128], kT[:, kb0 * 128 + a0:kb0 * 128 + a1],
                                                        start=True, stop=True),
                               reads=["qT", "kT"], writes=[("pss", s, a0)])
                        rk = [("pss", s, 0)] + ([("pss", s, 512)] if nk > 512 else [])
                        op("dve", lambda g: g.scalar_tensor_tensor(st[:, s, :nk], pss[:, s, :nk], scale, bm[:, j0:j0 + nk], ALU.mult, ALU.add),
                           reads=rk + ["bm"], writes=[("s", s)])
                        op("dve", lambda g: g.tensor_reduce(sm[:, s, 0:1], st[:, s, :nk], AX.X, ALU.max), reads=[("s", s)], writes=[("sm", s)])
                        op("dve", lambda g: g.tensor_scalar(sm[:, s, 1:2], sm[:, s, 0:1], -1.0, None, ALU.mult), reads=[("sm", s)], writes=[("sm", s)])
                        op("dve", lambda g: g.memset(sm[:, s, 2:3], 0.0), writes=[("sm2", s)])
                        op("act", lambda g: g.activation(out=pt_[:, s, :nk], in_=st[:, s, :nk], func=AF.Exp, bias=sm[:, s, 1:2], accum_out=sm[:, s, 2:3]),
                           reads=[("s", s), ("sm", s)], writes=[("p", s), ("sm2", s)])
                        op("dve", lambda g: g.reciprocal(sm[:, s, 3:4], sm[:, s, 2:3]), reads=[("sm2", s)], writes=[("sm3", s)])
                        for n in range(nkb):
                            op("pe", lambda g: g.transpose(ptp[:, s, n * 128:(n + 1) * 128], pt_[:, s, n * 128:(n + 1) * 128], ident[:]),
                               reads=[("p", s), "ident"], writes=[("ptp", s)], inc=(n == nkb - 1))
                        if s == 0:
                            op("act", lambda g: g.copy(pT[:, s, :nkb, :].rearrange("p n q -> p (n q)"), ptp[:, s, :nk]), reads=[("ptp", s)], writes=[("pT", s)])
                        else:
                            op("dve", lambda g: g.tensor_copy(pT[:, s, :nkb, :].rearrange("p n q -> p (n q)"), ptp[:, s, :nk]), reads=[("ptp", s)], writes=[("pT", s)])
                        for n in range(nkb):
                            op("pe", lambda g: g.matmul(po[:, s, 0:128], pT[:, s, n, :], vt[:, kb0 + n, :], start=(n == 0), stop=(n == nkb - 1)),
                               reads=[("pT", s), "v"], writes=[("po", s)], inc=(n == nkb - 1))
                        op("dve", lambda g: g.scalar_tensor_tensor(og[:, qi, :], po[:, s, 0:128], sm[:, s, 3:4], zt[:, qi, :], ALU.mult, ALU.mult),
                           reads=[("po", s), ("sm3", s), "z"], writes=["og"])
                    for pi in range(S // PS_):
                        tp = tb0 + pi * PS_
                        nsl = slice(pi * (PS_ // 128), (pi + 1) * (PS_ // 128))
                        dma("pool", YN.rows(tp, PS_)[:, h * 128:(h + 1) * 128].rearrange("(n p) d -> p n d", p=128), og[:, nsl, :], reads=["og"], key="ogst")
        cx.barrier()

    import os as _os2
    _nx = int(_os2.environ.get("K_EXTRA_SEMS", "0"))
    if _nx:
        with ExitStack() as es:
            dm = es.enter_context(sbt("dummy", [128, 64], F32))
            for j in range(_nx):
                dma("sp", dm[:, 0:16], x_in[0:128, 0:16], writes=["dummy"], key=("xs_dummy", j))
            cx.barrier()
    op("dve", lambda g: g.memset(epsb[:], c.EPS), writes=["epsb"])
    phase_mod()
    for i in range(n_layers_a):
        phase_convert(ssm_w_in[i], D, IN, 512, W_IN[i])
        phase_convert(ssm_w_out[i], DI, D, 256, W_OUT[i])
    if n_layers_b > 0:
        phase_convert(w_kv, D, 2 * AD, 512, W_KV)
        for i in range(n_layers_b):
            phase_convert(att_w_in[i], D, 2 * AD, 512, W_AIN[i])
            phase_convert(att_w_out[i], AD, D, 512, W_AOUT[i])

    import os as _os
    SKIP = set(_os.environ.get("K_SKIP", "").split(","))
    xcur = x_in
    for i in range(n_layers_a):
        layer = i
        run_gemm("g1", D, IN, 512, W_IN[i], min(1024, S), norm_src_factory(xcur, layer, True), out_store_f32(U))
        if "conv" not in SKIP:
            phase_conv(i)
        if "ssd" not in SKIP:
            phase_ssd(i)
        run_gemm("g2", DI, D, 256, W_OUT[i], min(512, S), load_src_factory(YN, DI), out_store_f32(Y2))
        phase_post(layer, xcur)
        xcur = out
    for i in range(n_layers_b):
        layer = n_layers_a + i
        if i == 0:
            run_gemm("g3", D, 2 * AD, 512, W_KV, min(1024, S), norm_src_factory(xcur, layer, False), out_store_split(KV, 2 * AD, None))
        run_gemm("g4", D, 2 * AD, 512, W_AIN[i], min(1024, S), norm_src_factory(xcur, layer, True), out_store_split(QS, AD, ZS))
        if "attn" not in SKIP:
            phase_attn(i)
        run_gemm("g5", AD, D, 512, W_AOUT[i], min(1024, S), load_src_factory(YN, AD), out_store_f32(Y2))
        phase_post(layer, xcur)
        xcur = out
    cx.barrier()
    return nc


def _bm_index():
    l = np.arange(128)[:, None]
    j = np.arange(640)[None, :]
    dist = l + 512 - j
    return np.clip(dist, -128, 128) + 128


def make_inputs(cfg, inputs):
    c = cfg
    f = lambda a: np.ascontiguousarray(np.asarray(a, dtype=np.float32))
    m = {
        "x": f(inputs["x"]).reshape(c.T, c.D),
        "c": f(inputs["c"]),
        "ada_w": f(inputs["ada_w"]), "ada_b": f(inputs["ada_b"]),
        "pre_norm_g": f(inputs["pre_norm_g"]), "post_norm_g": f(inputs["post_norm_g"]),
        "ssm_w_in": f(inputs["ssm_w_in"]), "ssm_conv_w": f(inputs["ssm_conv_w"]), "ssm_conv_b": f(inputs["ssm_conv_b"]),
        "ssm_dt_bias": f(inputs["ssm_dt_bias"]), "ssm_a_log": f(inputs["ssm_a_log"]), "ssm_d": f(inputs["ssm_d"]),
        "ssm_norm_g": f(inputs["ssm_norm_g"]), "ssm_w_out": f(inputs["ssm_w_out"]),
        "kv_norm_g": f(inputs["kv_norm_g"]), "w_kv": f(inputs["w_kv"]),
        "att_w_in": f(inputs["att_w_in"]), "att_w_out": f(inputs["att_w_out"]),
        "att_bm": f(np.asarray(inputs["att_rel_bias"])[:, :, _bm_index()]),
    }
    return m


def kernel(**inputs):
    nb = int(np.asarray(inputs["x"]).shape[0])
    cfg = Cfg(NB=1)
    nc = build(cfg)
    maps = []
    for b in range(nb):
        ib = dict(inputs)
        ib["x"] = np.asarray(inputs["x"])[b:b + 1]
        ib["c"] = np.asarray(inputs["c"])[b:b + 1]
        maps.append(make_inputs(cfg, ib))
    res = run_bass_kernel_spmd(nc, maps, core_ids=list(range(nb)))
    return np.stack([np.asarray(res.results[b]["out"], dtype=np.float32).reshape(cfg.S, cfg.D) for b in range(nb)], axis=0)
```

```python
import numpy as np
from contextlib import ExitStack
import concourse.bass as bass
import concourse.mybir as mybir
from concourse.bass_utils import run_bass_kernel_spmd

F32 = mybir.dt.float32
BF16 = mybir.dt.bfloat16
AF = mybir.ActivationFunctionType
ALU = mybir.AluOpType
AX = mybir.AxisListType

NEG = -30000.0
import os as _os0
NOSAME = bool(_os0.environ.get("K_NOSAME"))


class Cfg:
    def __init__(self, D=4096, S=8192, NB=2, G=8, PTOK=2048):
        self.D, self.S, self.NB = D, S, NB
        self.PTOK = PTOK
        self.T = NB * S
        self.DI = 2 * D
        self.P = 64
        self.H = self.DI // 64
        self.G = G
        self.R = self.H // self.G
        self.N = 128
        self.GC = self.R * 64
        self.CD = self.DI + 2 * self.G * self.N
        self.IN = self.DI + self.CD + self.H
        self.AH = D // 128
        self.AD = D
        self.KC = D // 128
        self.EPS = 1e-6


class Ctx:
    def __init__(self, nc):
        self.nc = nc
        self.eng = {"pe": nc.tensor, "act": nc.scalar, "dve": nc.vector, "pool": nc.gpsimd, "sp": nc.sync}
        self.esem = {}
        self.cnt = {}
        self.seen = {e: {} for e in self.eng}
        for e in self.eng:
            self.esem[e] = nc.alloc_semaphore("es_" + e)
            self.cnt[e] = 0
        self.dsem = {}
        self.dcnt = {}
        self.state = {}
        self.pool = []
        self.nsem = 0

    def _st(self, key):
        s = self.state.get(key)
        if s is None:
            s = {"w": None, "r": {}}
            self.state[key] = s
        return s

    def _wait(self, e, ev):
        if ev is None:
            return
        sem, val, src = ev
        if src == e and (e == "pe" or NOSAME):
            return
        sid = id(sem)
        if self.seen[e].get(sid, 0) >= val:
            return
        self.eng[e].wait_ge(sem, val)
        self.seen[e][sid] = val

    def _deps(self, e, reads, writes):
        for k in reads:
            self._wait(e, self._st(k)["w"])
        for k in writes:
            s = self._st(k)
            self._wait(e, s["w"])
            for ev in s["r"].values():
                self._wait(e, ev)

    def _commit(self, ev, reads, writes):
        for k in reads:
            s = self._st(k)
            old = s["r"].get(id(ev[0]))
            if old is None or old[1] < ev[1]:
                s["r"][id(ev[0])] = ev
        for k in writes:
            s = self._st(k)
            s["w"] = ev
            s["r"] = {}

    def op(self, e, fn, reads=(), writes=(), inc=True):
        self._deps(e, reads, writes)
        ins = fn(self.eng[e])
        if inc:
            self.cnt[e] += 1
            ins.then_inc(self.esem[e], 1)
            ev = (self.esem[e], self.cnt[e], e)
        else:
            ev = (self.esem[e], self.cnt[e] + 1, e)
        self._commit(ev, reads, writes)
        return ins

    def dma(self, q, out, in_, reads=(), writes=(), key=None, **kw):
        self._deps(q, reads, writes)
        if key is None:
            key = (tuple(writes) + tuple(reads))[0]
        if key not in self.dsem:
            if self.pool:
                self.dsem[key], self.dcnt[key] = self.pool.pop()
            else:
                self.nsem += 1
                self.dsem[key] = self.nc.alloc_semaphore("ds%d" % self.nsem)
                self.dcnt[key] = 0
        sem = self.dsem[key]
        self.dcnt[key] += 16
        self.eng[q].dma_start(out=out, in_=in_, **kw).then_inc(sem, 16)
        ev = (sem, self.dcnt[key], "dma")
        self._commit(ev, reads, writes)

    def barrier(self):
        evs = [(self.esem[e], self.cnt[e], "x") for e in self.eng if self.cnt[e] > 0]
        evs += [(self.dsem[k], self.dcnt[k], "dma") for k in self.dsem]
        for e in self.eng:
            for ev in evs:
                self._wait(e, ev)
        self.state = {}
        for k in list(self.dsem):
            self.pool.append((self.dsem[k], self.dcnt[k]))
        self.dsem = {}
        self.dcnt = {}


def _col_tiles(ncols, w):
    out = []
    c = 0
    while c < ncols:
        out.append((c, min(w, ncols - c)))
        c += w
    return out


def build(cfg, n_layers_a=2, n_layers_b=2, dbg=None):
    c = cfg
    D, T, S, KC = c.D, c.T, c.S, c.KC
    DI, CD, H, G, R, GC, IN = c.DI, c.CD, c.H, c.G, c.R, c.GC, c.IN
    AH, AD = c.AH, c.AD
    NL = n_layers_a + n_layers_b
    nc = bass.Bass("TRN2", target_bir_lowering=False)

    def din(name, shape, dt=F32):
        return nc.dram_tensor(name, list(shape), dt, kind="ExternalInput").ap()

    def dscr(name, shape, dt=F32):
        return nc.dram_tensor(name, list(shape), dt, kind=("ExternalOutput" if dbg else "Internal")).ap()

    x_in = din("x", [T, D])
    c_in = din("c", [c.NB, D])
    ada_w = din("ada_w", [4, D, 3 * D])
    ada_b = din("ada_b", [4, 3 * D])
    pre_g = din("pre_norm_g", [4, D])
    post_g = din("post_norm_g", [4, D])
    ssm_w_in = din("ssm_w_in", [2, D, IN])
    conv_w = din("ssm_conv_w", [2, 4, CD])
    conv_b = din("ssm_conv_b", [2, CD])
    dt_bias = din("ssm_dt_bias", [2, H])
    a_log = din("ssm_a_log", [2, H])
    ssm_d = din("ssm_d", [2, H])
    ssm_ng = din("ssm_norm_g", [2, G, GC])
    ssm_w_out = din("ssm_w_out", [2, DI, D])
    kv_g = din("kv_norm_g", [D])
    w_kv = din("w_kv", [D, 2 * AD])
    att_w_in = din("att_w_in", [2, D, 2 * AD])
    att_bm = din("att_bm", [2, AH, 128, 640])
    att_w_out = din("att_w_out", [2, AD, D])
    out = nc.dram_tensor("out", [T, D], F32, kind="ExternalOutput").ap()

    PTOK = c.PTOK

    class Pieces:
        def __init__(self, name, cols, dt=F32):
            self.t = [dscr("%s_p%d" % (name, i), [PTOK, cols], dt) for i in range(T // PTOK)]

        def rows(self, t0, n):
            p = t0 // PTOK
            o = t0 - p * PTOK
            assert o + n <= PTOK, (t0, n)
            return self.t[p][o:o + n]

    MODS = dscr("MODS", [4, 3, c.NB, D])
    U = Pieces("U", IN)
    XC = Pieces("XC", CD, BF16)
    DT = Pieces("DTs", H)
    YN = Pieces("YN", DI, BF16)
    Y2 = Pieces("Y2", D)
    KV = Pieces("KVs", 2 * AD, BF16)
    QS = Pieces("QS", AD, BF16)
    ZS = Pieces("ZS", AD)

    def wscr(name, K, ncols, w):
        return dscr(name, [len(_col_tiles(ncols, w)), 128, K // 128, w], BF16)

    W_IN = [wscr("W_IN%d" % i, D, IN, 512) for i in range(2)]
    W_OUT = [wscr("W_OUT%d" % i, DI, D, 256) for i in range(2)]
    W_KV = wscr("W_KV", D, 2 * AD, 512)
    W_AIN = [wscr("W_AIN%d" % i, D, 2 * AD, 512) for i in range(2)]
    W_AOUT = [wscr("W_AOUT%d" % i, AD, D, 512) for i in range(2)]

    cx = Ctx(nc)
    op, dma = cx.op, cx.dma
    uniq = [0]
    epsb = nc.alloc_sbuf_tensor("epsb", [128, 1], F32)

    def sbt(name, shape, dt):
        uniq[0] += 1
        return nc.sbuf_tensor("%s_%d" % (name, uniq[0]), shape, dt)

    def pst(name, shape, dt):
        uniq[0] += 1
        return nc.psum_tensor("%s_%d" % (name, uniq[0]), shape, dt)

    def phase_convert(src, K, ncols, w, dst):
        kcn = K // 128
        KB = min(8, kcn)
        with ExitStack() as es:
            cf = es.enter_context(sbt("cv_f", [128, 2, KB, w], F32))
            cb = es.enter_context(sbt("cv_b", [128, 2, KB, w], BF16))
            it = 0
            for ci, (c0, cw) in enumerate(_col_tiles(ncols, w)):
                for k0 in range(0, kcn, KB):
                    s = it % 2
                    srcap = src[k0 * 128:(k0 + KB) * 128, c0:c0 + cw].rearrange("(k p) n -> p k n", p=128)
                    dma("sp", cf[:, s, :, :cw], srcap, writes=[("cvf", s)])
                    e = ("act", "dve", "pool")[it % 3]
                    if e == "act":
                        op(e, lambda g: g.copy(cb[:, s, :, :cw], cf[:, s, :, :cw]), reads=[("cvf", s)], writes=[("cvb", s)])
                    else:
                        op(e, lambda g: g.tensor_copy(cb[:, s, :, :cw], cf[:, s, :, :cw]), reads=[("cvf", s)], writes=[("cvb", s)])
                    dma("sp", dst[ci, :, k0:k0 + KB, :cw], cb[:, s, :, :cw], reads=[("cvb", s)], key=("cvb_st", s))
                    it += 1
        cx.barrier()

    def phase_mod():
        with ExitStack() as es:
            cT = es.enter_context(sbt("m_ct", [128, c.NB, KC], F32))
            cs = es.enter_context(sbt("m_cs", [128, c.NB, KC], F32))
            wt = es.enter_context(sbt("m_w", [128, 2, 8, 512], F32))
            mod = es.enter_context(sbt("m_mod", [c.NB, 3 * D], F32))
            bt = es.enter_context(sbt("m_b", [c.NB, 3 * D], F32))
            gt = es.enter_context(sbt("m_g", [c.NB, 2, D], F32))
            ot = es.enter_context(sbt("m_o", [c.NB, 2, D], F32))
            ps = es.enter_context(pst("m_ps", [128, 2, 512], F32))
            with nc.allow_non_contiguous_dma(reason="tiny transposed load of c"):
                for b in range(c.NB):
                    dma("sp", cT[:, b, :], c_in[b].rearrange("(k p) -> p k", p=128), writes=["cT"], key=("cT", b))
            op("act", lambda g: g.activation(out=cs[:], in_=cT[:], func=AF.Silu), reads=["cT"], writes=["cs"])
            it = 0
            for l in range(NL):
                dma("sp", bt[:], ada_b[l].partition_broadcast(c.NB), writes=["mb"])
                dma("sp", gt[:, 0, :], pre_g[l].partition_broadcast(c.NB), writes=["mg0"])
                dma("sp", gt[:, 1, :], post_g[l].partition_broadcast(c.NB), writes=["mg1"])
                for ci, (c0, cw) in enumerate(_col_tiles(3 * D, 512)):
                    pb = ci % 2
                    for k0 in range(0, KC, 8):
                        kb = min(8, KC - k0)
                        s = it % 2
                        it += 1
                        dma("sp", wt[:, s, :kb, :], ada_w[l, k0 * 128:(k0 + kb) * 128, c0:c0 + cw].rearrange("(k p) n -> p k n", p=128),
                            writes=[("mw", s)])
                        for k in range(kb):
                            kk = k0 + k
                            op("pe", lambda g: g.matmul(ps[:c.NB, pb, :], cs[:, :, kk], wt[:, s, k, :], start=(kk == 0), stop=(kk == KC - 1)),
                               reads=["cs", ("mw", s)], writes=[("mps", pb)], inc=(k == kb - 1))
                    op("dve", lambda g: g.tensor_tensor(mod[:, c0:c0 + cw], ps[:c.NB, pb, :], bt[:, c0:c0 + cw], ALU.add),
                       reads=[("mps", pb), "mb"], writes=["mod"])
                op("dve", lambda g: g.scalar_tensor_tensor(ot[:, 0, :], mod[:, D:2 * D], 1.0, gt[:, 0, :], ALU.add, ALU.mult),
                   reads=["mod", "mg0"], writes=["mo0"])
                op("dve", lambda g: g.tensor_tensor(ot[:, 1, :], mod[:, 2 * D:3 * D], gt[:, 1, :], ALU.mult),
                   reads=["mod", "mg1"], writes=["mo1"])
                dma("sp", MODS[l, 0], ot[:, 0, :], reads=["mo0"], key="mst0")
                dma("sp", MODS[l, 1], mod[:, 0:D], reads=["mod"], key="mst1")
                dma("sp", MODS[l, 2], ot[:, 1, :], reads=["mo1"], key="mst2")
        cx.barrier()

    def phase_gemm(name, K, ncols, w, Wt, TG, src_fn, out_fn, src_tiles):
        kcn = K // 128
        cts = _col_tiles(ncols, w)
        TB = TG // 128
        with ExitStack() as es:
            hT = es.enter_context(sbt(name + "_hT", [128, kcn, TG], BF16))
            hb = es.enter_context(sbt(name + "_hb", [128, 2, K], BF16))
            wsb = es.enter_context(sbt(name + "_w", [128, 2, kcn, w], BF16))
            osb = es.enter_context(sbt(name + "_o", [128, 4, w], F32))
            ident = es.enter_context(sbt(name + "_id", [128, 128], BF16))
            identf = es.enter_context(sbt(name + "_idf", [128, 128], F32))
            pa = es.enter_context(pst(name + "_pa", [128, 4, 512], F32))
            pt = es.enter_context(pst(name + "_pt", [128, 2, 1024], BF16))
            make_ident(identf, ident)
            wit = 0
            oit = 0
            for tg in range(T // TG):
                for tb in range(TB):
                    slot = tb % 2
                    src_fn(tg * TB + tb, hb[:, slot, :], slot, src_tiles)
                    KT = min(8, kcn)
                    for k0 in range(0, kcn, KT):
                        pb = (k0 // KT) % 2
                        for k in range(KT):
                            op("pe", lambda g: g.transpose(pt[:, pb, k * 128:(k + 1) * 128], hb[:, slot, (k0 + k) * 128:(k0 + k + 1) * 128], ident[:]),
                               reads=[("hb", slot), "ident"], writes=[("pt", pb)], inc=(k == KT - 1))
                        dst = hT[:, k0:k0 + KT, tb * 128:(tb + 1) * 128]
                        srcp = pt[:, pb, :KT * 128].rearrange("p (k t) -> p k t", k=KT)
                        if pb == 0:
                            op("act", lambda g: g.copy(dst, srcp), reads=[("pt", pb)], writes=[("hT", tb)])
                        else:
                            op("dve", lambda g: g.tensor_copy(dst, srcp), reads=[("pt", pb)], writes=[("hT", tb)])
                for ci, (c0, cw) in enumerate(cts):
                    ws = wit % 2
                    wit += 1
                    dma("sp", wsb[:, ws, :, :cw], Wt[ci, :, :, :cw], writes=[("w", ws)])
                    for tb in range(TB):
                        pb = oit % 4
                        oit += 1
                        for k in range(kcn):
                            op("pe", lambda g: g.matmul(pa[:, pb, :cw], hT[:, k, tb * 128:(tb + 1) * 128], wsb[:, ws, k, :cw],
                                                        start=(k == 0), stop=(k == kcn - 1)),
                               reads=[("hT", tb), ("w", ws)], writes=[("pa", pb)], inc=(k == kcn - 1))
                        out_fn(tg * TB + tb, c0, cw, pa[:, pb, :cw], ("pa", pb), osb[:, pb, :cw], ("osb", pb))
        cx.barrier()

    def make_ident(identf, ident):
        op("pool", lambda g: g.memset(identf[:], 1.0), writes=["identf"])
        op("pool", lambda g: g.affine_select(identf[:], identf[:], [[-1, 128]], ALU.is_equal, 0.0, base=0, channel_multiplier=1),
           reads=["identf"], writes=["identf"])
        op("pool", lambda g: g.tensor_copy(ident[:], identf[:]), reads=["identf"], writes=["ident"])

    def norm_src_factory(xsrc, layer, use_mod):
        def alloc():
            return (sbt("ns_x", [128, 1, D], F32), sbt("ns_g", [128, D], F32),
                    sbt("ns_s", [128, D], F32), sbt("ns_ss", [128, 1, 2], F32),
                    sbt("ns_h", [128, 2], F32))
        state = {"b": -1, "i": 0}

        def fn(tbg, dst, slot, tiles):
            xt, gm, sh, ss, hf = tiles
            b = (tbg * 128) // S
            if b != state["b"]:
                state["b"] = b
                if use_mod:
                    dma("sp", gm[:], MODS[layer, 0, b].partition_broadcast(128), writes=["gm"])
                    dma("sp", sh[:], MODS[layer, 1, b].partition_broadcast(128), writes=["sh"])
                else:
                    dma("sp", gm[:], kv_g.partition_broadcast(128), writes=["gm"])
            xs = 0
            dma("sp", xt[:, xs, :], xsrc[tbg * 128:(tbg + 1) * 128, :], writes=[("nx", xs)])
            op("dve", lambda g: g.memset(ss[:, xs, :], 0.0), writes=[("nss", xs)])
            op("act", lambda g: g.activation(out=dst, in_=xt[:, xs, :], func=AF.Square, accum_out=ss[:, xs, 0:1]),
               reads=[("nx", xs)], writes=[("hb", slot), ("nss", xs)])
            op("act", lambda g: g.activation(out=ss[:, xs, 1:2], in_=ss[:, xs, 0:1], func=AF.Ln, bias=epsb[:], scale=1.0 / D),
               reads=[("nss", xs), "epsb"], writes=[("nss", xs)])
            op("act", lambda g: g.activation(out=ss[:, xs, 1:2], in_=ss[:, xs, 1:2], func=AF.Exp, scale=-0.5),
               reads=[("nss", xs)], writes=[("nss", xs)])
            if use_mod:
                op("dve", lambda g: g.scalar_tensor_tensor(xt[:, xs, :], xt[:, xs, :], ss[:, xs, 1:2], gm[:], ALU.mult, ALU.mult),
                   reads=[("nx", xs), ("nss", xs), "gm"], writes=[("nx", xs)])
                op("pool", lambda g: g.tensor_tensor(dst, xt[:, xs, :], sh[:], ALU.add), reads=[("nx", xs), "sh"], writes=[("hb", slot)])
            else:
                op("dve", lambda g: g.scalar_tensor_tensor(dst, xt[:, xs, :], ss[:, xs, 1:2], gm[:], ALU.mult, ALU.mult),
                   reads=[("nx", xs), ("nss", xs), "gm"], writes=[("hb", slot)])
        return alloc, fn

    def load_src_factory(src, ncol):
        def alloc():
            return ()

        def fn(tbg, dst, slot, tiles):
            dma("sp", dst, src.rows(tbg * 128, 128)[:, 0:ncol], writes=[("hb", slot)])
        return alloc, fn

    def run_gemm(name, K, ncols, w, Wt, TG, src_factory, out_fn):
        alloc, fn = src_factory
        tl = alloc()
        if len(tl) == 0:
            phase_gemm(name, K, ncols, w, Wt, TG, fn, out_fn, ())
        else:
            with ExitStack() as es2:
                aa = tuple(es2.enter_context(t) for t in tl)
                phase_gemm(name, K, ncols, w, Wt, TG, fn, out_fn, aa)

    def out_store_f32(dst):
        def fn(tbg, c0, cw, ps, pskey, osb, okey):
            e = "act" if (tbg + c0 // 128) % 2 == 0 else "dve"
            if e == "act":
                op("act", lambda g: g.copy(osb, ps), reads=[pskey], writes=[okey])
            else:
                op("dve", lambda g: g.tensor_copy(osb, ps), reads=[pskey], writes=[okey])
            dma("pool", dst.rows(tbg * 128, 128)[:, c0:c0 + cw], osb, reads=[okey], key=("ost", okey))
        return fn

    def out_store_split(dst_b, nb_cols, dst_f):
        def fn(tbg, c0, cw, ps, pskey, osb, okey):
            if c0 < nb_cols:
                ob = osb.bitcast(BF16)[:, :cw]
                op("act", lambda g: g.copy(ob, ps), reads=[pskey], writes=[okey])
                dma("pool", dst_b.rows(tbg * 128, 128)[:, c0:c0 + cw], ob, reads=[okey], key=("ost", okey))
            else:
                op("dve", lambda g: g.tensor_copy(osb, ps), reads=[pskey], writes=[okey])
                dma("pool", dst_f.rows(tbg * 128, 128)[:, c0 - nb_cols:c0 - nb_cols + cw], osb, reads=[okey], key=("ost", okey))
        return fn

    def phase_conv(i):
        CW = 512
        with ExitStack() as es:
            cw_t = es.enter_context(sbt("cv_w", [128, 4, CW], F32))
            cb_t = es.enter_context(sbt("cv_bi", [128, CW], F32))
            ut = es.enter_context(sbt("cv_u", [128, 2, 4, CW], F32))
            mt = es.enter_context(sbt("cv_m", [128, 4, CW], F32))
            ot = es.enter_context(sbt("cv_o", [128, 2, CW], BF16))
            it = 0
            for c0, cw in _col_tiles(CD, CW):
                for k in range(4):
                    dma("sp", cw_t[:, k, :cw], conv_w[i, k, c0:c0 + cw].partition_broadcast(128), writes=[("cw", k)])
                dma("sp", cb_t[:, :cw], conv_b[i, c0:c0 + cw].partition_broadcast(128), writes=["cb"])
                for tb in range(T // 128):
                    s = it % 2
                    it += 1
                    t0 = tb * 128
                    first = (t0 % S == 0)
                    for k in range(4):
                        sh = 3 - k
                        if first and sh > 0:
                            op("pool", lambda g: g.memset(ut[:, s, k, :cw], 0.0), writes=[("u", s, k)])
                            dma("sp", ut[sh:128, s, k, :cw], U.rows(t0, 128 - sh)[:, DI + c0:DI + c0 + cw], writes=[("u", s, k)])
                        elif sh > 0 and t0 % PTOK == 0:
                            dma("sp", ut[0:sh, s, k, :cw], U.rows(t0 - sh, sh)[:, DI + c0:DI + c0 + cw], writes=[("u", s, k)])
                            dma("sp", ut[sh:128, s, k, :cw], U.rows(t0, 128 - sh)[:, DI + c0:DI + c0 + cw], writes=[("u", s, k)])
                        else:
                            dma("sp", ut[:, s, k, :cw], U.rows(t0 - sh, 128)[:, DI + c0:DI + c0 + cw], writes=[("u", s, k)])
                    for k in range(4):
                        e = "dve" if k % 2 == 0 else "pool"
                        op(e, lambda g: g.tensor_tensor(mt[:, k, :cw], ut[:, s, k, :cw], cw_t[:, k, :cw], ALU.mult),
                           reads=[("u", s, k), ("cw", k)], writes=[("m", k)])
                    op("dve", lambda g: g.tensor_tensor(mt[:, 0, :cw], mt[:, 0, :cw], mt[:, 2, :cw], ALU.add),
                       reads=[("m", 0), ("m", 2)], writes=[("m", 0)])
                    op("pool", lambda g: g.tensor_tensor(mt[:, 1, :cw], mt[:, 1, :cw], mt[:, 3, :cw], ALU.add),
                       reads=[("m", 1), ("m", 3)], writes=[("m", 1)])
                    op("pool", lambda g: g.tensor_tensor(mt[:, 1, :cw], mt[:, 1, :cw], cb_t[:, :cw], ALU.add),
                       reads=[("m", 1), "cb"], writes=[("m", 1)])
                    op("dve", lambda g: g.tensor_tensor(mt[:, 0, :cw], mt[:, 0, :cw], mt[:, 1, :cw], ALU.add),
                       reads=[("m", 0), ("m", 1)], writes=[("m", 0)])
                    op("act", lambda g: g.activation(out=ot[:, s, :cw], in_=mt[:, 0, :cw], func=AF.Silu),
                       reads=[("m", 0)], writes=[("co", s)])
                    dma("pool", XC.rows(t0, 128)[:, c0:c0 + cw], ot[:, s, :cw], reads=[("co", s)], key=("cost", s))
        cx.barrier()
        with ExitStack() as es:
            bt = es.enter_context(sbt("dt_b", [128, H], F32))
            ut = es.enter_context(sbt("dt_u", [128, 2, H], F32))
            at = es.enter_context(sbt("dt_a", [128, 2, H], F32))
            ot = es.enter_context(sbt("dt_o", [128, 2, H], F32))
            dma("sp", bt[:], dt_bias[i].partition_broadcast(128), writes=["dtb"])
            for tb in range(T // 128):
                s = tb % 2
                t0 = tb * 128
                dma("sp", ut[:, s, :], U.rows(t0, 128)[:, DI + CD:DI + CD + H], writes=[("dtu", s)])
                op("dve", lambda g: g.tensor_tensor(ut[:, s, :], ut[:, s, :], bt[:], ALU.add), reads=[("dtu", s), "dtb"], writes=[("dtu", s)])
                op("dve", lambda g: g.tensor_scalar(at[:, s, :], ut[:, s, :], -1.0, None, ALU.mult), reads=[("dtu", s)], writes=[("dta", s)])
                op("dve", lambda g: g.tensor_tensor(at[:, s, :], at[:, s, :], ut[:, s, :], ALU.min), reads=[("dtu", s), ("dta", s)], writes=[("dta", s)])
                op("act", lambda g: g.activation(out=at[:, s, :], in_=at[:, s, :], func=AF.Exp), reads=[("dta", s)], writes=[("dta", s)])
                op("act", lambda g: g.activation(out=at[:, s, :], in_=at[:, s, :], func=AF.Ln, bias=1.0), reads=[("dta", s)], writes=[("dta", s)])
                op("dve", lambda g: g.scalar_tensor_tensor(ot[:, s, :], ut[:, s, :], 0.0, at[:, s, :], ALU.max, ALU.add),
                   reads=[("dtu", s), ("dta", s)], writes=[("dto", s)])
                dma("pool", DT.rows(t0, 128), ot[:, s, :], reads=[("dto", s)], key=("dtst", s))
        cx.barrier()

    def phase_ssd(i):
        L = 64
        RL = R * L
        NH2 = (GC + 511) // 512
        with ExitStack() as es:
            xs = es.enter_context(sbt("s_xs", [L, 2, GC], BF16))
            bc = es.enter_context(sbt("s_bc", [L, 2, 256], BF16))
            dtt = es.enter_context(sbt("s_dt", [L, 2, R], F32))
            zt = es.enter_context(sbt("s_z", [L, 2, GC], F32))
            arep = es.enter_context(sbt("s_A", [128, H], F32))
            drep = es.enter_context(sbt("s_D", [128, H], F32))
            ngt = es.enter_context(sbt("s_ng", [128, GC], F32))
            tri = es.enter_context(sbt("s_tri", [L, L], F32))
            u01 = es.enter_context(sbt("s_u01", [L, L], F32))
            ones = es.enter_context(sbt("s_one", [L, 128], F32))
            identf = es.enter_context(sbt("s_idf", [128, 128], F32))
            ident = es.enter_context(sbt("s_id", [128, 128], BF16))
            dtA = es.enter_context(sbt("s_dtA", [L, R], F32))
            Rm = es.enter_context(sbt("s_Rm", [L, R, L], F32))
            Et = es.enter_context(sbt("s_E", [L, R, L], BF16))
            BCT = es.enter_context(sbt("s_BT", [128, 2, 2, L], BF16))
            cbm = es.enter_context(sbt("s_cbm", [L, L], F32))
            Wt_ = es.enter_context(sbt("s_W", [L, 2, R, L], BF16))
            xdt = es.enter_context(sbt("s_xdt", [L, 2, R, 64], BF16))
            xe = es.enter_context(sbt("s_xe", [L, 2, R, 64], BF16))
            ex = es.enter_context(sbt("s_ex", [128, 2, 3, R], F32))
            ysb = es.enter_context(sbt("s_y", [L, GC], F32))
            t1 = es.enter_context(sbt("s_t1", [L, GC], F32))
            t2 = es.enter_context(sbt("s_t2", [L, GC], F32))
            sz = es.enter_context(sbt("s_sz", [L, GC], F32))
            ss = es.enter_context(sbt("s_ss", [L, 2], F32))
            yn = es.enter_context(sbt("s_yn", [L, 2, GC], BF16))
            ST = es.enter_context(sbt("s_ST", [128, GC], F32))
            STb = es.enter_context(sbt("s_STb", [128, GC], BF16))
            p01 = es.enter_context(pst("s_p01", [128, 2, 512], F32))
            p23 = es.enter_context(pst("s_p23", [128, 2, 512], F32))
            p45 = es.enter_context(pst("s_p45", [128, 2, 512], F32))
            p6 = es.enter_context(pst("s_p6", [128, 1024], BF16))
            p7 = es.enter_context(pst("s_p7", [128, 512], F32))
            make_ident(identf, ident)
            op("pool", lambda g: g.memset(tri[:], 1.0), writes=["tri"])
            op("pool", lambda g: g.affine_select(tri[:], tri[:], [[1, L]], ALU.is_ge, 0.0, base=0, channel_multiplier=-1),
               reads=["tri"], writes=["tri"])
            op("pool", lambda g: g.memset(u01[:], 1.0), writes=["u01"])
            op("pool", lambda g: g.affine_select(u01[:], u01[:], [[-1, L]], ALU.is_gt, 0.0, base=0, channel_multiplier=1),
               reads=["u01"], writes=["u01"])
            op("pool", lambda g: g.memset(ones[:], 1.0), writes=["ones"])
            dma("sp", arep[:], a_log[i].partition_broadcast(128), writes=["arep"])
            op("act", lambda g: g.activation(out=arep[:], in_=arep[:], func=AF.Exp), reads=["arep"], writes=["arep"])
            op("dve", lambda g: g.tensor_scalar(arep[:], arep[:], -1.0, None, ALU.mult), reads=["arep"], writes=["arep"])
            dma("sp", drep[:], ssm_d[i].partition_broadcast(128), writes=["drep"])
            yit = [0]

            def front(b, gI, ch):
                s = ch % 2
                t0 = b * S + ch * L
                dma("sp", xs[:, s, :], XC.rows(t0, L)[:, gI * GC:(gI + 1) * GC], writes=[("xs", s)])
                dma("sp", bc[:, s, 0:128], XC.rows(t0, L)[:, DI + gI * 128:DI + (gI + 1) * 128], writes=[("bcB", s)])
                dma("sp", bc[:, s, 128:256], XC.rows(t0, L)[:, DI + G * 128 + gI * 128:DI + G * 128 + (gI + 1) * 128], writes=[("bcC", s)])
                dma("sp", dtt[:, s, :], DT.rows(t0, L)[:, gI * R:(gI + 1) * R], writes=[("dt", s)])
                dma("sp", zt[:, s, :], U.rows(t0, L)[:, gI * GC:(gI + 1) * GC], writes=[("z", s)])
                xs3 = xs[:, s, :].rearrange("p (r q) -> p r q", r=R)
                op("dve", lambda g: g.tensor_tensor(dtA[:], dtt[:, s, :], arep[:L, gI * R:(gI + 1) * R], ALU.mult),
                   reads=[("dt", s), "arep"], writes=["dtA"])
                op("dve", lambda g: g.tensor_tensor(Rm[:], tri[:].unsqueeze(1).to_broadcast([L, R, L]),
                                                    dtA[:].unsqueeze(2).to_broadcast([L, R, L]), ALU.mult),
                   reads=["tri", "dtA"], writes=["Rm"])
                Rm2 = Rm[:].rearrange("p r l -> p (r l)")
                for hh in range((RL + 511) // 512):
                    n0, n1 = hh * 512, min(RL, (hh + 1) * 512)
                    op("pe", lambda g: g.matmul(p01[:L, hh, :n1 - n0], u01[:], Rm2[:, n0:n1], start=True, stop=True),
                       reads=["u01", "Rm"], writes=[("p01", hh)])
                op("pe", lambda g: g.matmul(p7[:L, 0:R], tri[:], dtA[:], start=True, stop=True), reads=["tri", "dtA"], writes=["p7"], inc=False)
                op("pe", lambda g: g.matmul(p7[:L, 64:64 + R], u01[:], dtA[:], start=True, stop=True), reads=["u01", "dtA"], writes=["p7"], inc=False)
                op("pe", lambda g: g.matmul(p7[:, 128:128 + R], ones[:], dtA[:], start=True, stop=True), reads=["ones", "dtA"], writes=["p7"])
                op("act", lambda g: g.activation(out=ex[:L, s, 0, :], in_=p7[:L, 0:R], func=AF.Exp), reads=["p7"], writes=[("ex0", s)])
                op("act", lambda g: g.activation(out=ex[:L, s, 1, :], in_=p7[:L, 64:64 + R], func=AF.Exp), reads=["p7"], writes=[("ex1", s)])
                op("act", lambda g: g.activation(out=ex[:, s, 2, :], in_=p7[:, 128:128 + R], func=AF.Exp), reads=["p7"], writes=[("ex2", s)])
                for hh in range((RL + 511) // 512):
                    n0, n1 = hh * 512, min(RL, (hh + 1) * 512)
                    op("act", lambda g: g.activation(out=Et[:].rearrange("p r l -> p (r l)")[:, n0:n1], in_=p01[:L, hh, :n1 - n0], func=AF.Exp),
                       reads=[("p01", hh)], writes=["E"])
                op("pe", lambda g: g.transpose(p6[:, 0:L], bc[:, s, 0:128], ident[:L, :L]), reads=[("bcB", s), "ident"], writes=["p6a"])
                op("pe", lambda g: g.transpose(p6[:, L:2 * L], bc[:, s, 128:256], ident[:L, :L]), reads=[("bcC", s), "ident"], writes=["p6b"])
                op("act", lambda g: g.copy(BCT[:, s, 0, :], p6[:, 0:L]), reads=["p6a"], writes=[("BT", s)])
                op("dve", lambda g: g.tensor_copy(BCT[:, s, 1, :], p6[:, L:2 * L]), reads=["p6b"], writes=[("CT", s)])
                op("pe", lambda g: g.matmul(p7[:L, 256:256 + L], BCT[:, s, 0, :], BCT[:, s, 1, :], start=True, stop=True),
                   reads=[("BT", s), ("CT", s)], writes=["p7c"])
                op("dve", lambda g: g.tensor_tensor(cbm[:], p7[:L, 256:256 + L], tri[:], ALU.mult), reads=["p7c", "tri"], writes=["cbm"])
                op("dve", lambda g: g.tensor_tensor(Wt_[:, s], Et[:], cbm[:].unsqueeze(1).to_broadcast([L, R, L]), ALU.mult),
                   reads=["E", "cbm"], writes=[("W", s)])
                op("pool", lambda g: g.tensor_tensor(xdt[:, s], xs3, dtt[:, s, :].unsqueeze(2).to_broadcast([L, R, 64]), ALU.mult),
                   reads=[("xs", s), ("dt", s)], writes=[("xdt", s)])
                op("pool", lambda g: g.tensor_tensor(xe[:, s], xdt[:, s], ex[:L, s, 1, :].unsqueeze(2).to_broadcast([L, R, 64]), ALU.mult),
                   reads=[("xdt", s), ("ex1", s)], writes=[("xe", s)])

            def back(b, gI, ch):
                s = ch % 2
                t0 = b * S + ch * L
                xs3 = xs[:, s, :].rearrange("p (r q) -> p r q", r=R)
                for r in range(R):
                    hh, off = (r * 64) // 512, (r * 64) % 512
                    op("pe", lambda g: g.matmul(p23[:L, hh, off:off + 64], Wt_[:, s, r, :], xdt[:, s, r, :], start=True, stop=True),
                       reads=[("W", s), ("xdt", s)], writes=[("p23", hh)], inc=(r == R - 1 or (r * 64 + 64) % 512 == 0))
                for hh in range(NH2):
                    n0, n1 = hh * 512, min(GC, (hh + 1) * 512)
                    op("pe", lambda g: g.matmul(p45[:L, hh, :n1 - n0], BCT[:, s, 1, :], STb[:, n0:n1], start=True, stop=True),
                       reads=[("CT", s), "STb"], writes=[("p45", hh)])
                xe2 = xe[:, s].rearrange("p r q -> p (r q)")
                for hh in range(NH2):
                    n0, n1 = hh * 512, min(GC, (hh + 1) * 512)
                    op("pe", lambda g: g.matmul(p01[:, hh, :n1 - n0], bc[:, s, 0:128], xe2[:, n0:n1], start=True, stop=True),
                       reads=[("bcB", s), ("xe", s)], writes=[("p01", hh)])
                op("dve", lambda g: g.tensor_tensor(ST[:].rearrange("p (r q) -> p r q", q=64), ST[:].rearrange("p (r q) -> p r q", q=64),
                                                    ex[:, s, 2, :].unsqueeze(2).to_broadcast([128, R, 64]), ALU.mult),
                   reads=["ST", ("ex2", s)], writes=["ST"])
                for hh in range(NH2):
                    n0, n1 = hh * 512, min(GC, (hh + 1) * 512)
                    op("dve", lambda g: g.tensor_tensor(ST[:, n0:n1], ST[:, n0:n1], p01[:, hh, :n1 - n0], ALU.add),
                       reads=["ST", ("p01", hh)], writes=["ST"])
                op("act", lambda g: g.copy(STb[:], ST[:]), reads=["ST"], writes=["STb"])
                for hh in range(NH2):
                    n0, n1 = hh * 512, min(GC, (hh + 1) * 512)
                    r0, r1 = n0 // 64, n1 // 64
                    op("act", lambda g: g.copy(ysb[:, n0:n1], p23[:L, hh, :n1 - n0]), reads=[("p23", hh)], writes=[("ysb", hh)])
                    op("dve", lambda g: g.tensor_tensor(t1[:, n0:n1].rearrange("p (r q) -> p r q", q=64),
                                                        p45[:L, hh, :n1 - n0].rearrange("p (r q) -> p r q", q=64),
                                                        ex[:L, s, 0, r0:r1].unsqueeze(2).to_broadcast([L, r1 - r0, 64]), ALU.mult),
                       reads=[("p45", hh), ("ex0", s)], writes=[("t1", hh)])
                    op("dve", lambda g: g.tensor_tensor(t1[:, n0:n1], t1[:, n0:n1], ysb[:, n0:n1], ALU.add),
                       reads=[("t1", hh), ("ysb", hh)], writes=[("t1", hh)])
                op("pool", lambda g: g.tensor_tensor(t2[:].rearrange("p (r q) -> p r q", q=64), xs3,
                                                     drep[:L, gI * R:(gI + 1) * R].unsqueeze(2).to_broadcast([L, R, 64]), ALU.mult),
                   reads=[("xs", s), "drep"], writes=["t2"])
                t1keys = [("t1", hh) for hh in range(NH2)]
                op("pool", lambda g: g.tensor_tensor(t2[:], t2[:], t1[:], ALU.add), reads=["t2"] + t1keys, writes=["t2"])
                op("act", lambda g: g.activation(out=sz[:], in_=zt[:, s, :], func=AF.Silu), reads=[("z", s)], writes=["sz"])
                op("dve", lambda g: g.tensor_tensor(t2[:], t2[:], sz[:], ALU.mult), reads=["t2", "sz"], writes=["t2"])
                op("dve", lambda g: g.memset(ss[:], 0.0), writes=["ss"])
                op("act", lambda g: g.activation(out=sz[:], in_=t2[:], func=AF.Square, accum_out=ss[:, 0:1]),
                   reads=["t2"], writes=["sz", "ss"])
                op("act", lambda g: g.activation(out=ss[:, 1:2], in_=ss[:, 0:1], func=AF.Ln, bias=epsb[:L, :], scale=1.0 / GC), reads=["ss", "epsb"], writes=["ss"])
                op("act", lambda g: g.activation(out=ss[:, 1:2], in_=ss[:, 1:2], func=AF.Exp, scale=-0.5), reads=["ss"], writes=["ss"])
                ys = yit[0] % 2
                yit[0] += 1
                op("dve", lambda g: g.scalar_tensor_tensor(yn[:, ys, :], t2[:], ss[:, 1:2], ngt[:L, :], ALU.mult, ALU.mult),
                   reads=["t2", "ss", "ngt"], writes=[("yn", ys)])
                dma("pool", YN.rows(t0, L)[:, gI * GC:(gI + 1) * GC], yn[:, ys, :], reads=[("yn", ys)], key=("ynst", ys))

            NCH = S // L
            for b in range(c.NB):
                for gI in range(G):
                    dma("sp", ngt[:], ssm_ng[i, gI].partition_broadcast(128), writes=["ngt"])
                    op("dve", lambda g: g.memset(ST[:], 0.0), writes=["ST"])
                    op("pool", lambda g: g.memset(STb[:], 0.0), writes=["STb"])
                    for ch in range(NCH + 1):
                        if ch < NCH:
                            front(b, gI, ch)
                        if ch > 0:
                            back(b, gI, ch - 1)
        cx.barrier()

    def phase_post(layer, xsrc):
        with ExitStack() as es:
            xt = es.enter_context(sbt("pn_x", [128, 2, D], F32))
            yt = es.enter_context(sbt("pn_y", [128, 2, D], F32))
            gp = es.enter_context(sbt("pn_g", [128, D], F32))
            junk = es.enter_context(sbt("pn_j", [128, D], F32))
            ss = es.enter_context(sbt("pn_ss", [128, 2, 2], F32))
            ot = es.enter_context(sbt("pn_o", [128, 2, D], F32))
            for tb in range(T // 128):
                s = tb % 2
                t0 = tb * 128
                if t0 % S == 0:
                    dma("sp", gp[:], MODS[layer, 2, t0 // S].partition_broadcast(128), writes=["gp"])
                dma("sp", xt[:, s, :], xsrc[t0:t0 + 128, :], writes=[("px", s)])
                dma("sp", yt[:, s, :], Y2.rows(t0, 128), writes=[("py", s)])
                op("dve", lambda g: g.memset(ss[:, s, :], 0.0), writes=[("pss", s)])
                op("act", lambda g: g.activation(out=junk[:], in_=yt[:, s, :], func=AF.Square, accum_out=ss[:, s, 0:1]),
                   reads=[("py", s)], writes=["junk", ("pss", s)])
                op("act", lambda g: g.activation(out=ss[:, s, 1:2], in_=ss[:, s, 0:1], func=AF.Ln, bias=epsb[:], scale=1.0 / D), reads=[("pss", s), "epsb"], writes=[("pss", s)])
                op("act", lambda g: g.activation(out=ss[:, s, 1:2], in_=ss[:, s, 1:2], func=AF.Exp, scale=-0.5), reads=[("pss", s)], writes=[("pss", s)])
                op("dve", lambda g: g.scalar_tensor_tensor(yt[:, s, :], yt[:, s, :], ss[:, s, 1:2], gp[:], ALU.mult, ALU.mult),
                   reads=[("py", s), ("pss", s), "gp"], writes=[("py", s)])
                op("pool", lambda g: g.tensor_tensor(ot[:, s, :], yt[:, s, :], xt[:, s, :], ALU.add), reads=[("py", s), ("px", s)], writes=[("po", s)])
                dma("pool", out[t0:t0 + 128, :], ot[:, s, :], reads=[("po", s)], key=("post", s))
        cx.barrier()

    def phase_attn(i):
        NQ = S // 128
        scale = 128 ** -0.5
        with ExitStack() as es:
            qt = es.enter_context(sbt("a_q", [128, NQ, 128], BF16))
            kt = es.enter_context(sbt("a_k", [128, NQ, 128], BF16))
            vt = es.enter_context(sbt("a_v", [128, NQ, 128], BF16))
            zt = es.enter_context(sbt("a_z", [128, NQ, 128], F32))
            qT = es.enter_context(sbt("a_qT", [128, NQ * 128], BF16))
            kT = es.enter_context(sbt("a_kT", [128, NQ * 128], BF16))
            og = es.enter_context(sbt("a_og", [128, NQ, 128], BF16))
            bm = es.enter_context(sbt("a_bm", [128, 640], F32))
            mk = es.enter_context(sbt("a_mk", [128, 640], F32))
            st = es.enter_context(sbt("a_s", [128, 2, 640], F32))
            pt_ = es.enter_context(sbt("a_p", [128, 2, 640], BF16))
            pT = es.enter_context(sbt("a_pT", [128, 2, 5, 128], BF16))
            sm = es.enter_context(sbt("a_st", [128, 2, 4], F32))
            ot = es.enter_context(sbt("a_o", [128, 2, 128], F32))
            ident = es.enter_context(sbt("a_id", [128, 128], BF16))
            identf = es.enter_context(sbt("a_idf", [128, 128], F32))
            pss = es.enter_context(pst("a_ps", [128, 2, 1024], F32))
            ptp = es.enter_context(pst("a_pt", [128, 2, 1024], BF16))
            po = es.enter_context(pst("a_po", [128, 2, 512], F32))
            make_ident(identf, ident)
            op("pool", lambda g: g.memset(mk[:], 0.0), writes=["mk"])
            op("pool", lambda g: g.memset(mk[0:64, 576:640], NEG), reads=[], writes=["mk"])
            op("pool", lambda g: g.memset(mk[64:128, 0:64], NEG), reads=[], writes=["mk"])
            it = 0
            for b in range(c.NB):
                for h in range(AH):
                    tb0 = b * S
                    PS_ = min(PTOK, S)
                    for pi in range(S // PS_):
                        tp = tb0 + pi * PS_
                        nsl = slice(pi * (PS_ // 128), (pi + 1) * (PS_ // 128))
                        dma("sp", qt[:, nsl, :], QS.rows(tp, PS_)[:, h * 128:(h + 1) * 128].rearrange("(n p) d -> p n d", p=128), writes=["q"], key="q")
                        dma("sp", kt[:, nsl, :], KV.rows(tp, PS_)[:, h * 128:(h + 1) * 128].rearrange("(n p) d -> p n d", p=128), writes=["k"], key="k")
                        dma("sp", vt[:, nsl, :], KV.rows(tp, PS_)[:, AD + h * 128:AD + (h + 1) * 128].rearrange("(n p) d -> p n d", p=128), writes=["v"], key="v")
                        dma("sp", zt[:, nsl, :], ZS.rows(tp, PS_)[:, h * 128:(h + 1) * 128].rearrange("(n p) d -> p n d", p=128), writes=["z"], key="z")
                    dma("sp", bm[:], att_bm[i, h], writes=["bm"])
                    op("pool", lambda g: g.tensor_tensor(bm[:], bm[:], mk[:], ALU.add), reads=["bm", "mk"], writes=["bm"])
                    op("act", lambda g: g.activation(out=zt[:], in_=zt[:], func=AF.Silu), reads=["z"], writes=["z"])
                    for src, dstT, nm in ((qt, qT, "qT"), (kt, kT, "kT")):
                        for n0 in range(0, NQ, 8):
                            nb = min(8, NQ - n0)
                            pb = (n0 // 8) % 2
                            for n in range(nb):
                                op("pe", lambda g: g.transpose(ptp[:, pb, n * 128:(n + 1) * 128], src[:, n0 + n, :], ident[:]),
                                   reads=["q" if nm == "qT" else "k", "ident"], writes=[("ptp", pb)], inc=(n == nb - 1))
                            if pb == 0:
                                op("act", lambda g: g.copy(dstT[:, n0 * 128:(n0 + nb) * 128], ptp[:, pb, :nb * 128]), reads=[("ptp", pb)], writes=[nm])
                            else:
                                op("dve", lambda g: g.tensor_copy(dstT[:, n0 * 128:(n0 + nb) * 128], ptp[:, pb, :nb * 128]), reads=[("ptp", pb)], writes=[nm])
                    def geom(qi):
                        kb0 = max(0, qi - 4)
                        nkb = qi - kb0 + 1
                        j0 = (kb0 - (qi - 4)) * 128
                        return kb0, nkb, j0, nkb * 128

                    def front(qi):
                        s = qi % 2
                        kb0, nkb, j0, nk = geom(qi)
                        for (a0, a1) in ((0, min(nk, 512)), (512, nk)):
                            if a1 <= a0:
                                continue
                            op("pe", lambda g: g.matmul(pss[:, s, a0:a1], qT[:, qi * 128:(qi + 1) * 128], kT[:, kb0 * 128 + a0:kb0 * 128 + a1],
                                                        start=True, stop=True),
                               reads=["qT", "kT"], writes=[("pss", s, a0)])
                        rk = [("pss", s, 0)] + ([("pss", s, 512)] if nk > 512 else [])
                        op("dve", lambda g: g.scalar_tensor_tensor(st[:, s, :nk], pss[:, s, :nk], scale, bm[:, j0:j0 + nk], ALU.mult, ALU.add),
                           reads=rk + ["bm"], writes=[("s", s)])
                        op("dve", lambda g: g.tensor_reduce(sm[:, s, 0:1], st[:, s, :nk], AX.X, ALU.max), reads=[("s", s)], writes=[("sm", s)])
                        op("dve", lambda g: g.tensor_scalar(sm[:, s, 1:2], sm[:, s, 0:1], -1.0, None, ALU.mult), reads=[("sm", s)], writes=[("sm", s)])
                        op("dve", lambda g: g.memset(sm[:, s, 2:3], 0.0), writes=[("sm2", s)])
                        op("act", lambda g: g.activation(out=pt_[:, s, :nk], in_=st[:, s, :nk], func=AF.Exp, bias=sm[:, s, 1:2], accum_out=sm[:, s, 2:3]),
                           reads=[("s", s), ("sm", s)], writes=[("p", s), ("sm2", s)])
                        op("dve", lambda g: g.reciprocal(sm[:, s, 3:4], sm[:, s, 2:3]), reads=[("sm2", s)], writes=[("sm3", s)])

                    def back(qi):
                        s = qi % 2
                        kb0, nkb, j0, nk = geom(qi)
                        for n in range(nkb):
                            op("pe", lambda g: g.transpose(ptp[:, s, n * 128:(n + 1) * 128], pt_[:, s, n * 128:(n + 1) * 128], ident[:]),
                               reads=[("p", s), "ident"], writes=[("ptp", s)], inc=(n == nkb - 1))
                        if s == 0:
                            op("act", lambda g: g.copy(pT[:, s, :nkb, :].rearrange("p n q -> p (n q)"), ptp[:, s, :nk]), reads=[("ptp", s)], writes=[("pT", s)])
                        else:
                            op("dve", lambda g: g.tensor_copy(pT[:, s, :nkb, :].rearrange("p n q -> p (n q)"), ptp[:, s, :nk]), reads=[("ptp", s)], writes=[("pT", s)])
                        for n in range(nkb):
                            op("pe", lambda g: g.matmul(po[:, s, 0:128], pT[:, s, n, :], vt[:, kb0 + n, :], start=(n == 0), stop=(n == nkb - 1)),
                               reads=[("pT", s), "v"], writes=[("po", s)], inc=(n == nkb - 1))
                        op("dve", lambda g: g.scalar_tensor_tensor(og[:, qi, :], po[:, s, 0:128], sm[:, s, 3:4], zt[:, qi, :], ALU.mult, ALU.mult),
                           reads=[("po", s), ("sm3", s), "z"], writes=["og"])

                    for qi in range(NQ + 1):
                        if qi < NQ:
                            front(qi)
                        if qi > 0:
                            back(qi - 1)
                    for pi in range(S // PS_):
                        tp = tb0 + pi * PS_
                        nsl = slice(pi * (PS_ // 128), (pi + 1) * (PS_ // 128))
                        dma("pool", YN.rows(tp, PS_)[:, h * 128:(h + 1) * 128].rearrange("(n p) d -> p n d", p=128), og[:, nsl, :], reads=["og"], key="ogst")
        cx.barrier()

    import os as _os2
    _nx = int(_os2.environ.get("K_EXTRA_SEMS", "0"))
    if _nx:
        with ExitStack() as es:
            dm = es.enter_context(sbt("dummy", [128, 64], F32))
            for j in range(_nx):
                dma("sp", dm[:, 0:16], x_in[0:128, 0:16], writes=["dummy"], key=("xs_dummy", j))
            cx.barrier()
    op("dve", lambda g: g.memset(epsb[:], c.EPS), writes=["epsb"])
    phase_mod()
    for i in range(n_layers_a):
        phase_convert(ssm_w_in[i], D, IN, 512, W_IN[i])
        phase_convert(ssm_w_out[i], DI, D, 256, W_OUT[i])
    if n_layers_b > 0:
        phase_convert(w_kv, D, 2 * AD, 512, W_KV)
        for i in range(n_layers_b):
            phase_convert(att_w_in[i], D, 2 * AD, 512, W_AIN[i])
            phase_convert(att_w_out[i], AD, D, 512, W_AOUT[i])

    import os as _os
    SKIP = set(_os.environ.get("K_SKIP", "").split(","))
    xcur = x_in
    for i in range(n_layers_a):
        layer = i
        run_gemm("g1", D, IN, 512, W_IN[i], min(1024, S), norm_src_factory(xcur, layer, True), out_store_f32(U))
        if "conv" not in SKIP:
            phase_conv(i)
        if "ssd" not in SKIP:
            phase_ssd(i)
        run_gemm("g2", DI, D, 256, W_OUT[i], min(512, S), load_src_factory(YN, DI), out_store_f32(Y2))
        phase_post(layer, xcur)
        xcur = out
    for i in range(n_layers_b):
        layer = n_layers_a + i
        if i == 0:
            run_gemm("g3", D, 2 * AD, 512, W_KV, min(1024, S), norm_src_factory(xcur, layer, False), out_store_split(KV, 2 * AD, None))
        run_gemm("g4", D, 2 * AD, 512, W_AIN[i], min(1024, S), norm_src_factory(xcur, layer, True), out_store_split(QS, AD, ZS))
        if "attn" not in SKIP:
            phase_attn(i)
        run_gemm("g5", AD, D, 512, W_AOUT[i], min(1024, S), load_src_factory(YN, AD), out_store_f32(Y2))
        phase_post(layer, xcur)
        xcur = out
    cx.barrier()
    return nc


def _bm_index():
    l = np.arange(128)[:, None]
    j = np.arange(640)[None, :]
    dist = l + 512 - j
    return np.clip(dist, -128, 128) + 128


def make_inputs(cfg, inputs):
    c = cfg
    f = lambda a: np.ascontiguousarray(np.asarray(a, dtype=np.float32))
    m = {
        "x": f(inputs["x"]).reshape(c.T, c.D),
        "c": f(inputs["c"]),
        "ada_w": f(inputs["ada_w"]), "ada_b": f(inputs["ada_b"]),
        "pre_norm_g": f(inputs["pre_norm_g"]), "post_norm_g": f(inputs["post_norm_g"]),
        "ssm_w_in": f(inputs["ssm_w_in"]), "ssm_conv_w": f(inputs["ssm_conv_w"]), "ssm_conv_b": f(inputs["ssm_conv_b"]),
        "ssm_dt_bias": f(inputs["ssm_dt_bias"]), "ssm_a_log": f(inputs["ssm_a_log"]), "ssm_d": f(inputs["ssm_d"]),
        "ssm_norm_g": f(inputs["ssm_norm_g"]), "ssm_w_out": f(inputs["ssm_w_out"]),
        "kv_norm_g": f(inputs["kv_norm_g"]), "w_kv": f(inputs["w_kv"]),
        "att_w_in": f(inputs["att_w_in"]), "att_w_out": f(inputs["att_w_out"]),
        "att_bm": f(np.asarray(inputs["att_rel_bias"])[:, :, _bm_index()]),
    }
    return m


def kernel(**inputs):
    nb = int(np.asarray(inputs["x"]).shape[0])
    cfg = Cfg(NB=1)
    nc = build(cfg)
    maps = []
    for b in range(nb):
        ib = dict(inputs)
        ib["x"] = np.asarray(inputs["x"])[b:b + 1]
        ib["c"] = np.asarray(inputs["c"])[b:b + 1]
        maps.append(make_inputs(cfg, ib))
    res = run_bass_kernel_spmd(nc, maps, core_ids=list(range(nb)))
    return np.stack([np.asarray(res.results[b]["out"], dtype=np.float32).reshape(cfg.S, cfg.D) for b in range(nb)], axis=0)
```

```python
import numpy as np
from contextlib import ExitStack
import concourse.bass as bass
import concourse.mybir as mybir
from concourse.bass_utils import run_bass_kernel_spmd

F32 = mybir.dt.float32
BF16 = mybir.dt.bfloat16
AF = mybir.ActivationFunctionType
ALU = mybir.AluOpType
AX = mybir.AxisListType

NEG = -30000.0
import os as _os0
NOSAME = bool(_os0.environ.get("K_NOSAME"))


class Cfg:
    def __init__(self, D=4096, S=8192, NB=2, G=8, PTOK=2048):
        self.D, self.S, self.NB = D, S, NB
        self.PTOK = PTOK
        self.T = NB * S
        self.DI = 2 * D
        self.P = 64
        self.H = self.DI // 64
        self.G = G
        self.R = self.H // self.G
        self.N = 128
        self.GC = self.R * 64
        self.CD = self.DI + 2 * self.G * self.N
        self.IN = self.DI + self.CD + self.H
        self.AH = D // 128
        self.AD = D
        self.KC = D // 128
        self.EPS = 1e-6


class Ctx:
    def __init__(self, nc):
        self.nc = nc
        self.eng = {"pe": nc.tensor, "act": nc.scalar, "dve": nc.vector, "pool": nc.gpsimd, "sp": nc.sync}
        self.esem = {}
        self.cnt = {}
        self.seen = {e: {} for e in self.eng}
        for e in self.eng:
            self.esem[e] = nc.alloc_semaphore("es_" + e)
            self.cnt[e] = 0
        self.dsem = {}
        self.dcnt = {}
        self.state = {}
        self.pool = []
        self.nsem = 0

    def _st(self, key):
        s = self.state.get(key)
        if s is None:
            s = {"w": None, "r": {}}
            self.state[key] = s
        return s

    def _wait(self, e, ev):
        if ev is None:
            return
        sem, val, src = ev
        if src == e and (e == "pe" or NOSAME):
            return
        sid = id(sem)
        if self.seen[e].get(sid, 0) >= val:
            return
        self.eng[e].wait_ge(sem, val)
        self.seen[e][sid] = val

    def _deps(self, e, reads, writes):
        for k in reads:
            self._wait(e, self._st(k)["w"])
        for k in writes:
            s = self._st(k)
            self._wait(e, s["w"])
            for ev in s["r"].values():
                self._wait(e, ev)

    def _commit(self, ev, reads, writes):
        for k in reads:
            s = self._st(k)
            old = s["r"].get(id(ev[0]))
            if old is None or old[1] < ev[1]:
                s["r"][id(ev[0])] = ev
        for k in writes:
            s = self._st(k)
            s["w"] = ev
            s["r"] = {}

    def op(self, e, fn, reads=(), writes=(), inc=True):
        self._deps(e, reads, writes)
        ins = fn(self.eng[e])
        if inc:
            self.cnt[e] += 1
            ins.then_inc(self.esem[e], 1)
            ev = (self.esem[e], self.cnt[e], e)
        else:
            ev = (self.esem[e], self.cnt[e] + 1, e)
        self._commit(ev, reads, writes)
        return ins

    def dma(self, q, out, in_, reads=(), writes=(), key=None, **kw):
        self._deps(q, reads, writes)
        if key is None:
            key = (tuple(writes) + tuple(reads))[0]
        if key not in self.dsem:
            if self.pool:
                self.dsem[key], self.dcnt[key] = self.pool.pop()
            else:
                self.nsem += 1
                self.dsem[key] = self.nc.alloc_semaphore("ds%d" % self.nsem)
                self.dcnt[key] = 0
        sem = self.dsem[key]
        self.dcnt[key] += 16
        self.eng[q].dma_start(out=out, in_=in_, **kw).then_inc(sem, 16)
        ev = (sem, self.dcnt[key], "dma")
        self._commit(ev, reads, writes)

    def barrier(self):
        evs = [(self.esem[e], self.cnt[e], "x") for e in self.eng if self.cnt[e] > 0]
        evs += [(self.dsem[k], self.dcnt[k], "dma") for k in self.dsem]
        for e in self.eng:
            for ev in evs:
                self._wait(e, ev)
        self.state = {}
        for k in list(self.dsem):
            self.pool.append((self.dsem[k], self.dcnt[k]))
        self.dsem = {}
        self.dcnt = {}


def _col_tiles(ncols, w):
    out = []
    c = 0
    while c < ncols:
        out.append((c, min(w, ncols - c)))
        c += w
    return out


def build(cfg, n_layers_a=2, n_layers_b=2, dbg=None):
    c = cfg
    D, T, S, KC = c.D, c.T, c.S, c.KC
    DI, CD, H, G, R, GC, IN = c.DI, c.CD, c.H, c.G, c.R, c.GC, c.IN
    AH, AD = c.AH, c.AD
    NL = n_layers_a + n_layers_b
    nc = bass.Bass("TRN2", target_bir_lowering=False)

    def din(name, shape, dt=F32):
        return nc.dram_tensor(name, list(shape), dt, kind="ExternalInput").ap()

    def dscr(name, shape, dt=F32):
        return nc.dram_tensor(name, list(shape), dt, kind=("ExternalOutput" if dbg else "Internal")).ap()

    x_in = din("x", [T, D])
    c_in = din("c", [c.NB, D])
    ada_w = din("ada_w", [4, D, 3 * D])
    ada_b = din("ada_b", [4, 3 * D])
    pre_g = din("pre_norm_g", [4, D])
    post_g = din("post_norm_g", [4, D])
    ssm_w_in = din("ssm_w_in", [2, D, IN])
    conv_w = din("ssm_conv_w", [2, 4, CD])
    conv_b = din("ssm_conv_b", [2, CD])
    dt_bias = din("ssm_dt_bias", [2, H])
    a_log = din("ssm_a_log", [2, H])
    ssm_d = din("ssm_d", [2, H])
    ssm_ng = din("ssm_norm_g", [2, G, GC])
    ssm_w_out = din("ssm_w_out", [2, DI, D])
    kv_g = din("kv_norm_g", [D])
    w_kv = din("w_kv", [D, 2 * AD])
    att_w_in = din("att_w_in", [2, D, 2 * AD])
    att_bm = din("att_bm", [2, AH, 128, 640])
    att_w_out = din("att_w_out", [2, AD, D])
    out = nc.dram_tensor("out", [T, D], F32, kind="ExternalOutput").ap()

    PTOK = c.PTOK

    class Pieces:
        def __init__(self, name, cols, dt=F32):
            self.t = [dscr("%s_p%d" % (name, i), [PTOK, cols], dt) for i in range(T // PTOK)]

        def rows(self, t0, n):
            p = t0 // PTOK
            o = t0 - p * PTOK
            assert o + n <= PTOK, (t0, n)
            return self.t[p][o:o + n]

    MODS = dscr("MODS", [4, 3, c.NB, D])
    U = Pieces("U", IN)
    XC = Pieces("XC", CD, BF16)
    DT = Pieces("DTs", H)
    YN = Pieces("YN", DI, BF16)
    Y2 = Pieces("Y2", D)
    KV = Pieces("KVs", 2 * AD, BF16)
    QS = Pieces("QS", AD, BF16)
    ZS = Pieces("ZS", AD)

    def wscr(name, K, ncols, w):
        return dscr(name, [len(_col_tiles(ncols, w)), 128, K // 128, w], BF16)

    W_IN = [wscr("W_IN%d" % i, D, IN, 512) for i in range(2)]
    W_OUT = [wscr("W_OUT%d" % i, DI, D, 256) for i in range(2)]
    W_KV = wscr("W_KV", D, 2 * AD, 512)
    W_AIN = [wscr("W_AIN%d" % i, D, 2 * AD, 512) for i in range(2)]
    W_AOUT = [wscr("W_AOUT%d" % i, AD, D, 512) for i in range(2)]

    cx = Ctx(nc)
    op, dma = cx.op, cx.dma
    uniq = [0]
    epsb = nc.alloc_sbuf_tensor("epsb", [128, 1], F32)

    def sbt(name, shape, dt):
        uniq[0] += 1
        return nc.sbuf_tensor("%s_%d" % (name, uniq[0]), shape, dt)

    def pst(name, shape, dt):
        uniq[0] += 1
        return nc.psum_tensor("%s_%d" % (name, uniq[0]), shape, dt)

    def phase_convert(src, K, ncols, w, dst):
        kcn = K // 128
        KB = min(8, kcn)
        with ExitStack() as es:
            cf = es.enter_context(sbt("cv_f", [128, 2, KB, w], F32))
            cb = es.enter_context(sbt("cv_b", [128, 2, KB, w], BF16))
            it = 0
            for ci, (c0, cw) in enumerate(_col_tiles(ncols, w)):
                for k0 in range(0, kcn, KB):
                    s = it % 2
                    srcap = src[k0 * 128:(k0 + KB) * 128, c0:c0 + cw].rearrange("(k p) n -> p k n", p=128)
                    dma("sp", cf[:, s, :, :cw], srcap, writes=[("cvf", s)])
                    e = ("act", "dve", "pool")[it % 3]
                    if e == "act":
                        op(e, lambda g: g.copy(cb[:, s, :, :cw], cf[:, s, :, :cw]), reads=[("cvf", s)], writes=[("cvb", s)])
                    else:
                        op(e, lambda g: g.tensor_copy(cb[:, s, :, :cw], cf[:, s, :, :cw]), reads=[("cvf", s)], writes=[("cvb", s)])
                    dma("sp", dst[ci, :, k0:k0 + KB, :cw], cb[:, s, :, :cw], reads=[("cvb", s)], key=("cvb_st", s))
                    it += 1
        cx.barrier()

    def phase_mod():
        with ExitStack() as es:
            cT = es.enter_context(sbt("m_ct", [128, c.NB, KC], F32))
            cs = es.enter_context(sbt("m_cs", [128, c.NB, KC], F32))
            wt = es.enter_context(sbt("m_w", [128, 2, 8, 512], F32))
            mod = es.enter_context(sbt("m_mod", [c.NB, 3 * D], F32))
            bt = es.enter_context(sbt("m_b", [c.NB, 3 * D], F32))
            gt = es.enter_context(sbt("m_g", [c.NB, 2, D], F32))
            ot = es.enter_context(sbt("m_o", [c.NB, 2, D], F32))
            ps = es.enter_context(pst("m_ps", [128, 2, 512], F32))
            with nc.allow_non_contiguous_dma(reason="tiny transposed load of c"):
                for b in range(c.NB):
                    dma("sp", cT[:, b, :], c_in[b].rearrange("(k p) -> p k", p=128), writes=["cT"], key=("cT", b))
            op("act", lambda g: g.activation(out=cs[:], in_=cT[:], func=AF.Silu), reads=["cT"], writes=["cs"])
            it = 0
            for l in range(NL):
                dma("sp", bt[:], ada_b[l].partition_broadcast(c.NB), writes=["mb"])
                dma("sp", gt[:, 0, :], pre_g[l].partition_broadcast(c.NB), writes=["mg0"])
                dma("sp", gt[:, 1, :], post_g[l].partition_broadcast(c.NB), writes=["mg1"])
                for ci, (c0, cw) in enumerate(_col_tiles(3 * D, 512)):
                    pb = ci % 2
                    for k0 in range(0, KC, 8):
                        kb = min(8, KC - k0)
                        s = it % 2
                        it += 1
                        dma("sp", wt[:, s, :kb, :], ada_w[l, k0 * 128:(k0 + kb) * 128, c0:c0 + cw].rearrange("(k p) n -> p k n", p=128),
                            writes=[("mw", s)])
                        for k in range(kb):
                            kk = k0 + k
                            op("pe", lambda g: g.matmul(ps[:c.NB, pb, :], cs[:, :, kk], wt[:, s, k, :], start=(kk == 0), stop=(kk == KC - 1)),
                               reads=["cs", ("mw", s)], writes=[("mps", pb)], inc=(k == kb - 1))
                    op("dve", lambda g: g.tensor_tensor(mod[:, c0:c0 + cw], ps[:c.NB, pb, :], bt[:, c0:c0 + cw], ALU.add),
                       reads=[("mps", pb), "mb"], writes=["mod"])
                op("dve", lambda g: g.scalar_tensor_tensor(ot[:, 0, :], mod[:, D:2 * D], 1.0, gt[:, 0, :], ALU.add, ALU.mult),
                   reads=["mod", "mg0"], writes=["mo0"])
                op("dve", lambda g: g.tensor_tensor(ot[:, 1, :], mod[:, 2 * D:3 * D], gt[:, 1, :], ALU.mult),
                   reads=["mod", "mg1"], writes=["mo1"])
                dma("sp", MODS[l, 0], ot[:, 0, :], reads=["mo0"], key="mst0")
                dma("sp", MODS[l, 1], mod[:, 0:D], reads=["mod"], key="mst1")
                dma("sp", MODS[l, 2], ot[:, 1, :], reads=["mo1"], key="mst2")
        cx.barrier()

    def phase_gemm(name, K, ncols, w, Wt, TG, src_fn, out_fn, src_tiles):
        kcn = K // 128
        cts = _col_tiles(ncols, w)
        TB = TG // 128
        with ExitStack() as es:
            hT = es.enter_context(sbt(name + "_hT", [128, kcn, TG], BF16))
            hb = es.enter_context(sbt(name + "_hb", [128, 2, K], BF16))
            wsb = es.enter_context(sbt(name + "_w", [128, 2, kcn, w], BF16))
            osb = es.enter_context(sbt(name + "_o", [128, 4, w], F32))
            ident = es.enter_context(sbt(name + "_id", [128, 128], BF16))
            identf = es.enter_context(sbt(name + "_idf", [128, 128], F32))
            pa = es.enter_context(pst(name + "_pa", [128, 4, 512], F32))
            pt = es.enter_context(pst(name + "_pt", [128, 2, 1024], BF16))
            make_ident(identf, ident)
            wit = 0
            oit = 0
            for tg in range(T // TG):
                for tb in range(TB):
                    slot = tb % 2
                    src_fn(tg * TB + tb, hb[:, slot, :], slot, src_tiles)
                    KT = min(8, kcn)
                    for k0 in range(0, kcn, KT):
                        pb = (k0 // KT) % 2
                        for k in range(KT):
                            op("pe", lambda g: g.transpose(pt[:, pb, k * 128:(k + 1) * 128], hb[:, slot, (k0 + k) * 128:(k0 + k + 1) * 128], ident[:]),
                               reads=[("hb", slot), "ident"], writes=[("pt", pb)], inc=(k == KT - 1))
                        dst = hT[:, k0:k0 + KT, tb * 128:(tb + 1) * 128]
                        srcp = pt[:, pb, :KT * 128].rearrange("p (k t) -> p k t", k=KT)
                        if pb == 0:
                            op("act", lambda g: g.copy(dst, srcp), reads=[("pt", pb)], writes=[("hT", tb)])
                        else:
                            op("dve", lambda g: g.tensor_copy(dst, srcp), reads=[("pt", pb)], writes=[("hT", tb)])
                for ci, (c0, cw) in enumerate(cts):
                    ws = wit % 2
                    wit += 1
                    dma("sp", wsb[:, ws, :, :cw], Wt[ci, :, :, :cw], writes=[("w", ws)])
                    for tb in range(TB):
                        pb = oit % 4
                        oit += 1
                        for k in range(kcn):
                            op("pe", lambda g: g.matmul(pa[:, pb, :cw], hT[:, k, tb * 128:(tb + 1) * 128], wsb[:, ws, k, :cw],
                                                        start=(k == 0), stop=(k == kcn - 1)),
                               reads=[("hT", tb), ("w", ws)], writes=[("pa", pb)], inc=(k == kcn - 1))
                        out_fn(tg * TB + tb, c0, cw, pa[:, pb, :cw], ("pa", pb), osb[:, pb, :cw], ("osb", pb))
        cx.barrier()

    def make_ident(identf, ident):
        op("pool", lambda g: g.memset(identf[:], 1.0), writes=["identf"])
        op("pool", lambda g: g.affine_select(identf[:], identf[:], [[-1, 128]], ALU.is_equal, 0.0, base=0, channel_multiplier=1),
           reads=["identf"], writes=["identf"])
        op("pool", lambda g: g.tensor_copy(ident[:], identf[:]), reads=["identf"], writes=["ident"])

    def norm_src_factory(xsrc, layer, use_mod):
        def alloc():
            return (sbt("ns_x", [128, 1, D], F32), sbt("ns_g", [128, D], F32),
                    sbt("ns_s", [128, D], F32), sbt("ns_ss", [128, 1, 2], F32),
                    sbt("ns_h", [128, 2], F32))
        state = {"b": -1, "i": 0}

        def fn(tbg, dst, slot, tiles):
            xt, gm, sh, ss, hf = tiles
            b = (tbg * 128) // S
            if b != state["b"]:
                state["b"] = b
                if use_mod:
                    dma("sp", gm[:], MODS[layer, 0, b].partition_broadcast(128), writes=["gm"])
                    dma("sp", sh[:], MODS[layer, 1, b].partition_broadcast(128), writes=["sh"])
                else:
                    dma("sp", gm[:], kv_g.partition_broadcast(128), writes=["gm"])
            xs = 0
            dma("sp", xt[:, xs, :], xsrc[tbg * 128:(tbg + 1) * 128, :], writes=[("nx", xs)])
            op("dve", lambda g: g.memset(ss[:, xs, :], 0.0), writes=[("nss", xs)])
            op("act", lambda g: g.activation(out=dst, in_=xt[:, xs, :], func=AF.Square, accum_out=ss[:, xs, 0:1]),
               reads=[("nx", xs)], writes=[("hb", slot), ("nss", xs)])
            op("act", lambda g: g.activation(out=ss[:, xs, 1:2], in_=ss[:, xs, 0:1], func=AF.Ln, bias=epsb[:], scale=1.0 / D),
               reads=[("nss", xs), "epsb"], writes=[("nss", xs)])
            op("act", lambda g: g.activation(out=ss[:, xs, 1:2], in_=ss[:, xs, 1:2], func=AF.Exp, scale=-0.5),
               reads=[("nss", xs)], writes=[("nss", xs)])
            if use_mod:
                op("dve", lambda g: g.scalar_tensor_tensor(xt[:, xs, :], xt[:, xs, :], ss[:, xs, 1:2], gm[:], ALU.mult, ALU.mult),
                   reads=[("nx", xs), ("nss", xs), "gm"], writes=[("nx", xs)])
                op("pool", lambda g: g.tensor_tensor(dst, xt[:, xs, :], sh[:], ALU.add), reads=[("nx", xs), "sh"], writes=[("hb", slot)])
            else:
                op("dve", lambda g: g.scalar_tensor_tensor(dst, xt[:, xs, :], ss[:, xs, 1:2], gm[:], ALU.mult, ALU.mult),
                   reads=[("nx", xs), ("nss", xs), "gm"], writes=[("hb", slot)])
        return alloc, fn

    def load_src_factory(src, ncol):
        def alloc():
            return ()

        def fn(tbg, dst, slot, tiles):
            dma("sp", dst, src.rows(tbg * 128, 128)[:, 0:ncol], writes=[("hb", slot)])
        return alloc, fn

    def run_gemm(name, K, ncols, w, Wt, TG, src_factory, out_fn):
        alloc, fn = src_factory
        tl = alloc()
        if len(tl) == 0:
            phase_gemm(name, K, ncols, w, Wt, TG, fn, out_fn, ())
        else:
            with ExitStack() as es2:
                aa = tuple(es2.enter_context(t) for t in tl)
                phase_gemm(name, K, ncols, w, Wt, TG, fn, out_fn, aa)

    def out_store_f32(dst):
        def fn(tbg, c0, cw, ps, pskey, osb, okey):
            e = "act" if (tbg + c0 // 128) % 2 == 0 else "dve"
            if e == "act":
                op("act", lambda g: g.copy(osb, ps), reads=[pskey], writes=[okey])
            else:
                op("dve", lambda g: g.tensor_copy(osb, ps), reads=[pskey], writes=[okey])
            dma("pool", dst.rows(tbg * 128, 128)[:, c0:c0 + cw], osb, reads=[okey], key=("ost", okey))
        return fn

    def out_store_split(dst_b, nb_cols, dst_f):
        def fn(tbg, c0, cw, ps, pskey, osb, okey):
            if c0 < nb_cols:
                ob = osb.bitcast(BF16)[:, :cw]
                op("act", lambda g: g.copy(ob, ps), reads=[pskey], writes=[okey])
                dma("pool", dst_b.rows(tbg * 128, 128)[:, c0:c0 + cw], ob, reads=[okey], key=("ost", okey))
            else:
                op("dve", lambda g: g.tensor_copy(osb, ps), reads=[pskey], writes=[okey])
                dma("pool", dst_f.rows(tbg * 128, 128)[:, c0 - nb_cols:c0 - nb_cols + cw], osb, reads=[okey], key=("ost", okey))
        return fn

    def phase_conv(i):
        CW = 512
        with ExitStack() as es:
            cw_t = es.enter_context(sbt("cv_w", [128, 4, CW], F32))
            cb_t = es.enter_context(sbt("cv_bi", [128, CW], F32))
            ut = es.enter_context(sbt("cv_u", [128, 2, 4, CW], F32))
            mt = es.enter_context(sbt("cv_m", [128, 4, CW], F32))
            ot = es.enter_context(sbt("cv_o", [128, 2, CW], BF16))
            it = 0
            for c0, cw in _col_tiles(CD, CW):
                for k in range(4):
                    dma("sp", cw_t[:, k, :cw], conv_w[i, k, c0:c0 + cw].partition_broadcast(128), writes=[("cw", k)])
                dma("sp", cb_t[:, :cw], conv_b[i, c0:c0 + cw].partition_broadcast(128), writes=["cb"])
                for tb in range(T // 128):
                    s = it % 2
                    it += 1
                    t0 = tb * 128
                    first = (t0 % S == 0)
                    for k in range(4):
                        sh = 3 - k
                        if first and sh > 0:
                            op("pool", lambda g: g.memset(ut[:, s, k, :cw], 0.0), writes=[("u", s, k)])
                            dma("sp", ut[sh:128, s, k, :cw], U.rows(t0, 128 - sh)[:, DI + c0:DI + c0 + cw], writes=[("u", s, k)])
                        elif sh > 0 and t0 % PTOK == 0:
                            dma("sp", ut[0:sh, s, k, :cw], U.rows(t0 - sh, sh)[:, DI + c0:DI + c0 + cw], writes=[("u", s, k)])
                            dma("sp", ut[sh:128, s, k, :cw], U.rows(t0, 128 - sh)[:, DI + c0:DI + c0 + cw], writes=[("u", s, k)])
                        else:
                            dma("sp", ut[:, s, k, :cw], U.rows(t0 - sh, 128)[:, DI + c0:DI + c0 + cw], writes=[("u", s, k)])
                    for k in range(4):
                        e = "dve" if k % 2 == 0 else "pool"
                        op(e, lambda g: g.tensor_tensor(mt[:, k, :cw], ut[:, s, k, :cw], cw_t[:, k, :cw], ALU.mult),
                           reads=[("u", s, k), ("cw", k)], writes=[("m", k)])
                    op("dve", lambda g: g.tensor_tensor(mt[:, 0, :cw], mt[:, 0, :cw], mt[:, 2, :cw], ALU.add),
                       reads=[("m", 0), ("m", 2)], writes=[("m", 0)])
                    op("pool", lambda g: g.tensor_tensor(mt[:, 1, :cw], mt[:, 1, :cw], mt[:, 3, :cw], ALU.add),
                       reads=[("m", 1), ("m", 3)], writes=[("m", 1)])
                    op("pool", lambda g: g.tensor_tensor(mt[:, 1, :cw], mt[:, 1, :cw], cb_t[:, :cw], ALU.add),
                       reads=[("m", 1), "cb"], writes=[("m", 1)])
                    op("dve", lambda g: g.tensor_tensor(mt[:, 0, :cw], mt[:, 0, :cw], mt[:, 1, :cw], ALU.add),
                       reads=[("m", 0), ("m", 1)], writes=[("m", 0)])
                    op("act", lambda g: g.activation(out=ot[:, s, :cw], in_=mt[:, 0, :cw], func=AF.Silu),
                       reads=[("m", 0)], writes=[("co", s)])
                    dma("pool", XC.rows(t0, 128)[:, c0:c0 + cw], ot[:, s, :cw], reads=[("co", s)], key=("cost", s))
        cx.barrier()
        with ExitStack() as es:
            bt = es.enter_context(sbt("dt_b", [128, H], F32))
            ut = es.enter_context(sbt("dt_u", [128, 2, H], F32))
            at = es.enter_context(sbt("dt_a", [128, 2, H], F32))
            ot = es.enter_context(sbt("dt_o", [128, 2, H], F32))
            dma("sp", bt[:], dt_bias[i].partition_broadcast(128), writes=["dtb"])
            for tb in range(T // 128):
                s = tb % 2
                t0 = tb * 128
                dma("sp", ut[:, s, :], U.rows(t0, 128)[:, DI + CD:DI + CD + H], writes=[("dtu", s)])
                op("dve", lambda g: g.tensor_tensor(ut[:, s, :], ut[:, s, :], bt[:], ALU.add), reads=[("dtu", s), "dtb"], writes=[("dtu", s)])
                op("dve", lambda g: g.tensor_scalar(at[:, s, :], ut[:, s, :], -1.0, None, ALU.mult), reads=[("dtu", s)], writes=[("dta", s)])
                op("dve", lambda g: g.tensor_tensor(at[:, s, :], at[:, s, :], ut[:, s, :], ALU.min), reads=[("dtu", s), ("dta", s)], writes=[("dta", s)])
                op("act", lambda g: g.activation(out=at[:, s, :], in_=at[:, s, :], func=AF.Exp), reads=[("dta", s)], writes=[("dta", s)])
                op("act", lambda g: g.activation(out=at[:, s, :], in_=at[:, s, :], func=AF.Ln, bias=1.0), reads=[("dta", s)], writes=[("dta", s)])
                op("dve", lambda g: g.scalar_tensor_tensor(ot[:, s, :], ut[:, s, :], 0.0, at[:, s, :], ALU.max, ALU.add),
                   reads=[("dtu", s), ("dta", s)], writes=[("dto", s)])
                dma("pool", DT.rows(t0, 128), ot[:, s, :], reads=[("dto", s)], key=("dtst", s))
        cx.barrier()

    def phase_ssd(i):
        L = 64
        RL = R * L
        NH2 = (GC + 511) // 512
        with ExitStack() as es:
            xs = es.enter_context(sbt("s_xs", [L, 2, GC], BF16))
            bc = es.enter_context(sbt("s_bc", [L, 2, 256], BF16))
            dtt = es.enter_context(sbt("s_dt", [L, 2, R], F32))
            zt = es.enter_context(sbt("s_z", [L, 2, GC], F32))
            arep = es.enter_context(sbt("s_A", [128, H], F32))
            drep = es.enter_context(sbt("s_D", [128, H], F32))
            ngt = es.enter_context(sbt("s_ng", [128, GC], F32))
            tri = es.enter_context(sbt("s_tri", [L, L], F32))
            u01 = es.enter_context(sbt("s_u01", [L, L], F32))
            ones = es.enter_context(sbt("s_one", [L, 128], F32))
            identf = es.enter_context(sbt("s_idf", [128, 128], F32))
            ident = es.enter_context(sbt("s_id", [128, 128], BF16))
            dtA = es.enter_context(sbt("s_dtA", [L, R], F32))
            Rm = es.enter_context(sbt("s_Rm", [L, R, L], F32))
            Et = es.enter_context(sbt("s_E", [L, R, L], BF16))
            BCT = es.enter_context(sbt("s_BT", [128, 2, 2, L], BF16))
            cbm = es.enter_context(sbt("s_cbm", [L, L], F32))
            Wt_ = es.enter_context(sbt("s_W", [L, 2, R, L], BF16))
            xdt = es.enter_context(sbt("s_xdt", [L, 2, R, 64], BF16))
            xe = es.enter_context(sbt("s_xe", [L, 2, R, 64], BF16))
            ex = es.enter_context(sbt("s_ex", [128, 2, 3, R], F32))
            ysb = es.enter_context(sbt("s_y", [L, GC], F32))
            t1 = es.enter_context(sbt("s_t1", [L, GC], F32))
            t2 = es.enter_context(sbt("s_t2", [L, 2, GC], F32))
            sz = es.enter_context(sbt("s_sz", [L, 2, GC], F32))
            ss = es.enter_context(sbt("s_ss", [L, 2], F32))
            yn = es.enter_context(sbt("s_yn", [L, 2, GC], BF16))
            ST = es.enter_context(sbt("s_ST", [128, GC], F32))
            STb = es.enter_context(sbt("s_STb", [128, GC], BF16))
            p01 = es.enter_context(pst("s_p01", [128, 2, 512], F32))
            p23 = es.enter_context(pst("s_p23", [128, 2, 512], F32))
            p45 = es.enter_context(pst("s_p45", [128, 2, 512], F32))
            p6 = es.enter_context(pst("s_p6", [128, 1024], BF16))
            p7 = es.enter_context(pst("s_p7", [128, 512], F32))
            make_ident(identf, ident)
            op("pool", lambda g: g.memset(tri[:], 1.0), writes=["tri"])
            op("pool", lambda g: g.affine_select(tri[:], tri[:], [[1, L]], ALU.is_ge, 0.0, base=0, channel_multiplier=-1),
               reads=["tri"], writes=["tri"])
            op("pool", lambda g: g.memset(u01[:], 1.0), writes=["u01"])
            op("pool", lambda g: g.affine_select(u01[:], u01[:], [[-1, L]], ALU.is_gt, 0.0, base=0, channel_multiplier=1),
               reads=["u01"], writes=["u01"])
            op("pool", lambda g: g.memset(ones[:], 1.0), writes=["ones"])
            dma("sp", arep[:], a_log[i].partition_broadcast(128), writes=["arep"])
            op("act", lambda g: g.activation(out=arep[:], in_=arep[:], func=AF.Exp), reads=["arep"], writes=["arep"])
            op("dve", lambda g: g.tensor_scalar(arep[:], arep[:], -1.0, None, ALU.mult), reads=["arep"], writes=["arep"])
            dma("sp", drep[:], ssm_d[i].partition_broadcast(128), writes=["drep"])
            yit = [0]

            def front(b, gI, ch):
                s = ch % 2
                t0 = b * S + ch * L
                dma("sp", xs[:, s, :], XC.rows(t0, L)[:, gI * GC:(gI + 1) * GC], writes=[("xs", s)])
                dma("sp", bc[:, s, 0:128], XC.rows(t0, L)[:, DI + gI * 128:DI + (gI + 1) * 128], writes=[("bcB", s)])
                dma("sp", bc[:, s, 128:256], XC.rows(t0, L)[:, DI + G * 128 + gI * 128:DI + G * 128 + (gI + 1) * 128], writes=[("bcC", s)])
                dma("sp", dtt[:, s, :], DT.rows(t0, L)[:, gI * R:(gI + 1) * R], writes=[("dt", s)])
                dma("sp", zt[:, s, :], U.rows(t0, L)[:, gI * GC:(gI + 1) * GC], writes=[("z", s)])
                xs3 = xs[:, s, :].rearrange("p (r q) -> p r q", r=R)
                op("dve", lambda g: g.tensor_tensor(dtA[:], dtt[:, s, :], arep[:L, gI * R:(gI + 1) * R], ALU.mult),
                   reads=[("dt", s), "arep"], writes=["dtA"])
                op("dve", lambda g: g.tensor_tensor(Rm[:], tri[:].unsqueeze(1).to_broadcast([L, R, L]),
                                                    dtA[:].unsqueeze(2).to_broadcast([L, R, L]), ALU.mult),
                   reads=["tri", "dtA"], writes=["Rm"])
                Rm2 = Rm[:].rearrange("p r l -> p (r l)")
                for hh in range((RL + 511) // 512):
                    n0, n1 = hh * 512, min(RL, (hh + 1) * 512)
                    op("pe", lambda g: g.matmul(p01[:L, hh, :n1 - n0], u01[:], Rm2[:, n0:n1], start=True, stop=True),
                       reads=["u01", "Rm"], writes=[("p01", hh)])
                op("pe", lambda g: g.matmul(p7[:L, 0:R], tri[:], dtA[:], start=True, stop=True), reads=["tri", "dtA"], writes=["p7"], inc=False)
                op("pe", lambda g: g.matmul(p7[:L, 64:64 + R], u01[:], dtA[:], start=True, stop=True), reads=["u01", "dtA"], writes=["p7"], inc=False)
                op("pe", lambda g: g.matmul(p7[:, 128:128 + R], ones[:], dtA[:], start=True, stop=True), reads=["ones", "dtA"], writes=["p7"])
                op("act", lambda g: g.activation(out=ex[:L, s, 0, :], in_=p7[:L, 0:R], func=AF.Exp), reads=["p7"], writes=[("ex0", s)])
                op("act", lambda g: g.activation(out=ex[:L, s, 1, :], in_=p7[:L, 64:64 + R], func=AF.Exp), reads=["p7"], writes=[("ex1", s)])
                op("act", lambda g: g.activation(out=ex[:, s, 2, :], in_=p7[:, 128:128 + R], func=AF.Exp), reads=["p7"], writes=[("ex2", s)])
                for hh in range((RL + 511) // 512):
                    n0, n1 = hh * 512, min(RL, (hh + 1) * 512)
                    op("act", lambda g: g.activation(out=Et[:].rearrange("p r l -> p (r l)")[:, n0:n1], in_=p01[:L, hh, :n1 - n0], func=AF.Exp),
                       reads=[("p01", hh)], writes=["E"])
                op("pe", lambda g: g.transpose(p6[:, 0:L], bc[:, s, 0:128], ident[:L, :L]), reads=[("bcB", s), "ident"], writes=["p6a"])
                op("pe", lambda g: g.transpose(p6[:, L:2 * L], bc[:, s, 128:256], ident[:L, :L]), reads=[("bcC", s), "ident"], writes=["p6b"])
                op("act", lambda g: g.copy(BCT[:, s, 0, :], p6[:, 0:L]), reads=["p6a"], writes=[("BT", s)])
                op("dve", lambda g: g.tensor_copy(BCT[:, s, 1, :], p6[:, L:2 * L]), reads=["p6b"], writes=[("CT", s)])
                op("pe", lambda g: g.matmul(p7[:L, 256:256 + L], BCT[:, s, 0, :], BCT[:, s, 1, :], start=True, stop=True),
                   reads=[("BT", s), ("CT", s)], writes=["p7c"])
                op("dve", lambda g: g.tensor_tensor(cbm[:], p7[:L, 256:256 + L], tri[:], ALU.mult), reads=["p7c", "tri"], writes=["cbm"])
                op("dve", lambda g: g.tensor_tensor(Wt_[:, s], Et[:], cbm[:].unsqueeze(1).to_broadcast([L, R, L]), ALU.mult),
                   reads=["E", "cbm"], writes=[("W", s)])
                op("pool", lambda g: g.tensor_tensor(xdt[:, s], xs3, dtt[:, s, :].unsqueeze(2).to_broadcast([L, R, 64]), ALU.mult),
                   reads=[("xs", s), ("dt", s)], writes=[("xdt", s)])
                op("pool", lambda g: g.tensor_tensor(xe[:, s], xdt[:, s], ex[:L, s, 1, :].unsqueeze(2).to_broadcast([L, R, 64]), ALU.mult),
                   reads=[("xdt", s), ("ex1", s)], writes=[("xe", s)])
                op("pool", lambda g: g.tensor_tensor(t2[:, s, :].rearrange("p (r q) -> p r q", q=64), xs3,
                                                     drep[:L, gI * R:(gI + 1) * R].unsqueeze(2).to_broadcast([L, R, 64]), ALU.mult),
                   reads=[("xs", s), "drep"], writes=[("t2", s)])
                op("act", lambda g: g.activation(out=sz[:, s, :], in_=zt[:, s, :], func=AF.Silu), reads=[("z", s)], writes=[("sz", s)])

            def back(b, gI, ch):
                s = ch % 2
                t0 = b * S + ch * L
                xs3 = xs[:, s, :].rearrange("p (r q) -> p r q", r=R)
                for r in range(R):
                    hh, off = (r * 64) // 512, (r * 64) % 512
                    op("pe", lambda g: g.matmul(p23[:L, hh, off:off + 64], Wt_[:, s, r, :], xdt[:, s, r, :], start=True, stop=True),
                       reads=[("W", s), ("xdt", s)], writes=[("p23", hh)], inc=(r == R - 1 or (r * 64 + 64) % 512 == 0))
                for hh in range(NH2):
                    n0, n1 = hh * 512, min(GC, (hh + 1) * 512)
                    op("pe", lambda g: g.matmul(p45[:L, hh, :n1 - n0], BCT[:, s, 1, :], STb[:, n0:n1], start=True, stop=True),
                       reads=[("CT", s), "STb"], writes=[("p45", hh)])
                xe2 = xe[:, s].rearrange("p r q -> p (r q)")
                for hh in range(NH2):
                    n0, n1 = hh * 512, min(GC, (hh + 1) * 512)
                    op("pe", lambda g: g.matmul(p01[:, hh, :n1 - n0], bc[:, s, 0:128], xe2[:, n0:n1], start=True, stop=True),
                       reads=[("bcB", s), ("xe", s)], writes=[("p01", hh)])
                op("dve", lambda g: g.tensor_tensor(ST[:].rearrange("p (r q) -> p r q", q=64), ST[:].rearrange("p (r q) -> p r q", q=64),
                                                    ex[:, s, 2, :].unsqueeze(2).to_broadcast([128, R, 64]), ALU.mult),
                   reads=["ST", ("ex2", s)], writes=["ST"])
                for hh in range(NH2):
                    n0, n1 = hh * 512, min(GC, (hh + 1) * 512)
                    op("dve", lambda g: g.tensor_tensor(ST[:, n0:n1], ST[:, n0:n1], p01[:, hh, :n1 - n0], ALU.add),
                       reads=["ST", ("p01", hh)], writes=["ST"])
                op("act", lambda g: g.copy(STb[:], ST[:]), reads=["ST"], writes=["STb"])
                for hh in range(NH2):
                    n0, n1 = hh * 512, min(GC, (hh + 1) * 512)
                    r0, r1 = n0 // 64, n1 // 64
                    op("dve", lambda g: g.tensor_tensor(t1[:, n0:n1].rearrange("p (r q) -> p r q", q=64),
                                                        p45[:L, hh, :n1 - n0].rearrange("p (r q) -> p r q", q=64),
                                                        ex[:L, s, 0, r0:r1].unsqueeze(2).to_broadcast([L, r1 - r0, 64]), ALU.mult),
                       reads=[("p45", hh), ("ex0", s)], writes=[("t1", hh)])
                    op("dve", lambda g: g.tensor_tensor(t1[:, n0:n1], t1[:, n0:n1], p23[:L, hh, :n1 - n0], ALU.add),
                       reads=[("t1", hh), ("p23", hh)], writes=[("t1", hh)])
                t1keys = [("t1", hh) for hh in range(NH2)]
                op("dve", lambda g: g.tensor_tensor(t1[:], t1[:], t2[:, s, :], ALU.add), reads=t1keys + [("t2", s)], writes=t1keys)
                op("dve", lambda g: g.tensor_tensor(t1[:], t1[:], sz[:, s, :], ALU.mult), reads=t1keys + [("sz", s)], writes=t1keys)
                op("dve", lambda g: g.memset(ss[:], 0.0), writes=["ss"])
                op("act", lambda g: g.activation(out=ysb[:], in_=t1[:], func=AF.Square, accum_out=ss[:, 0:1]),
                   reads=t1keys, writes=["ysbj", "ss"])
                op("act", lambda g: g.activation(out=ss[:, 1:2], in_=ss[:, 0:1], func=AF.Ln, bias=epsb[:L, :], scale=1.0 / GC), reads=["ss", "epsb"], writes=["ss"])
                op("act", lambda g: g.activation(out=ss[:, 1:2], in_=ss[:, 1:2], func=AF.Exp, scale=-0.5), reads=["ss"], writes=["ss"])
                ys = yit[0] % 2
                yit[0] += 1
                op("dve", lambda g: g.scalar_tensor_tensor(yn[:, ys, :], t1[:], ss[:, 1:2], ngt[:L, :], ALU.mult, ALU.mult),
                   reads=t1keys + ["ss", "ngt"], writes=[("yn", ys)])
                dma("pool", YN.rows(t0, L)[:, gI * GC:(gI + 1) * GC], yn[:, ys, :], reads=[("yn", ys)], key=("ynst", ys))

            NCH = S // L
            for b in range(c.NB):
                for gI in range(G):
                    dma("sp", ngt[:], ssm_ng[i, gI].partition_broadcast(128), writes=["ngt"])
                    op("dve", lambda g: g.memset(ST[:], 0.0), writes=["ST"])
                    op("pool", lambda g: g.memset(STb[:], 0.0), writes=["STb"])
                    for ch in range(NCH + 1):
                        if ch < NCH:
                            front(b, gI, ch)
                        if ch > 0:
                            back(b, gI, ch - 1)
        cx.barrier()

    def phase_post(layer, xsrc):
        with ExitStack() as es:
            xt = es.enter_context(sbt("pn_x", [128, 2, D], F32))
            yt = es.enter_context(sbt("pn_y", [128, 2, D], F32))
            gp = es.enter_context(sbt("pn_g", [128, D], F32))
            junk = es.enter_context(sbt("pn_j", [128, D], F32))
            ss = es.enter_context(sbt("pn_ss", [128, 2, 2], F32))
            ot = es.enter_context(sbt("pn_o", [128, 2, D], F32))
            for tb in range(T // 128):
                s = tb % 2
                t0 = tb * 128
                if t0 % S == 0:
                    dma("sp", gp[:], MODS[layer, 2, t0 // S].partition_broadcast(128), writes=["gp"])
                dma("sp", xt[:, s, :], xsrc[t0:t0 + 128, :], writes=[("px", s)])
                dma("sp", yt[:, s, :], Y2.rows(t0, 128), writes=[("py", s)])
                op("dve", lambda g: g.memset(ss[:, s, :], 0.0), writes=[("pss", s)])
                op("act", lambda g: g.activation(out=junk[:], in_=yt[:, s, :], func=AF.Square, accum_out=ss[:, s, 0:1]),
                   reads=[("py", s)], writes=["junk", ("pss", s)])
                op("act", lambda g: g.activation(out=ss[:, s, 1:2], in_=ss[:, s, 0:1], func=AF.Ln, bias=epsb[:], scale=1.0 / D), reads=[("pss", s), "epsb"], writes=[("pss", s)])
                op("act", lambda g: g.activation(out=ss[:, s, 1:2], in_=ss[:, s, 1:2], func=AF.Exp, scale=-0.5), reads=[("pss", s)], writes=[("pss", s)])
                op("dve", lambda g: g.scalar_tensor_tensor(yt[:, s, :], yt[:, s, :], ss[:, s, 1:2], gp[:], ALU.mult, ALU.mult),
                   reads=[("py", s), ("pss", s), "gp"], writes=[("py", s)])
                op("pool", lambda g: g.tensor_tensor(ot[:, s, :], yt[:, s, :], xt[:, s, :], ALU.add), reads=[("py", s), ("px", s)], writes=[("po", s)])
                dma("pool", out[t0:t0 + 128, :], ot[:, s, :], reads=[("po", s)], key=("post", s))
        cx.barrier()

    def phase_attn(i):
        NQ = S // 128
        scale = 128 ** -0.5
        with ExitStack() as es:
            qt = es.enter_context(sbt("a_q", [128, NQ, 128], BF16))
            kt = es.enter_context(sbt("a_k", [128, NQ, 128], BF16))
            vt = es.enter_context(sbt("a_v", [128, NQ, 128], BF16))
            zt = es.enter_context(sbt("a_z", [128, NQ, 128], F32))
            qT = es.enter_context(sbt("a_qT", [128, NQ * 128], BF16))
            kT = es.enter_context(sbt("a_kT", [128, NQ * 128], BF16))
            og = es.enter_context(sbt("a_og", [128, NQ, 128], BF16))
            bm = es.enter_context(sbt("a_bm", [128, 640], F32))
            mk = es.enter_context(sbt("a_mk", [128, 640], F32))
            st = es.enter_context(sbt("a_s", [128, 2, 640], F32))
            pt_ = es.enter_context(sbt("a_p", [128, 2, 640], BF16))
            pT = es.enter_context(sbt("a_pT", [128, 2, 5, 128], BF16))
            sm = es.enter_context(sbt("a_st", [128, 2, 4], F32))
            ot = es.enter_context(sbt("a_o", [128, 2, 128], F32))
            ident = es.enter_context(sbt("a_id", [128, 128], BF16))
            identf = es.enter_context(sbt("a_idf", [128, 128], F32))
            pss = es.enter_context(pst("a_ps", [128, 2, 1024], F32))
            ptp = es.enter_context(pst("a_pt", [128, 2, 1024], BF16))
            po = es.enter_context(pst("a_po", [128, 2, 512], F32))
            make_ident(identf, ident)
            op("pool", lambda g: g.memset(mk[:], 0.0), writes=["mk"])
            op("pool", lambda g: g.memset(mk[0:64, 576:640], NEG), reads=[], writes=["mk"])
            op("pool", lambda g: g.memset(mk[64:128, 0:64], NEG), reads=[], writes=["mk"])
            it = 0
            for b in range(c.NB):
                for h in range(AH):
                    tb0 = b * S
                    PS_ = min(PTOK, S)
                    for pi in range(S // PS_):
                        tp = tb0 + pi * PS_
                        nsl = slice(pi * (PS_ // 128), (pi + 1) * (PS_ // 128))
                        dma("sp", qt[:, nsl, :], QS.rows(tp, PS_)[:, h * 128:(h + 1) * 128].rearrange("(n p) d -> p n d", p=128), writes=["q"], key="q")
                        dma("sp", kt[:, nsl, :], KV.rows(tp, PS_)[:, h * 128:(h + 1) * 128].rearrange("(n p) d -> p n d", p=128), writes=["k"], key="k")
                        dma("sp", vt[:, nsl, :], KV.rows(tp, PS_)[:, AD + h * 128:AD + (h + 1) * 128].rearrange("(n p) d -> p n d", p=128), writes=["v"], key="v")
                        dma("sp", zt[:, nsl, :], ZS.rows(tp, PS_)[:, h * 128:(h + 1) * 128].rearrange("(n p) d -> p n d", p=128), writes=["z"], key="z")
                    dma("sp", bm[:], att_bm[i, h], writes=["bm"])
                    op("pool", lambda g: g.tensor_tensor(bm[:], bm[:], mk[:], ALU.add), reads=["bm", "mk"], writes=["bm"])
                    op("act", lambda g: g.activation(out=zt[:], in_=zt[:], func=AF.Silu), reads=["z"], writes=["z"])
                    for src, dstT, nm in ((qt, qT, "qT"), (kt, kT, "kT")):
                        for n0 in range(0, NQ, 8):
                            nb = min(8, NQ - n0)
                            pb = (n0 // 8) % 2
                            for n in range(nb):
                                op("pe", lambda g: g.transpose(ptp[:, pb, n * 128:(n + 1) * 128], src[:, n0 + n, :], ident[:]),
                                   reads=["q" if nm == "qT" else "k", "ident"], writes=[("ptp", pb)], inc=(n == nb - 1))
                            if pb == 0:
                                op("act", lambda g: g.copy(dstT[:, n0 * 128:(n0 + nb) * 128], ptp[:, pb, :nb * 128]), reads=[("ptp", pb)], writes=[nm])
                            else:
                                op("dve", lambda g: g.tensor_copy(dstT[:, n0 * 128:(n0 + nb) * 128], ptp[:, pb, :nb * 128]), reads=[("ptp", pb)], writes=[nm])
                    def geom(qi):
                        kb0 = max(0, qi - 4)
                        nkb = qi - kb0 + 1
                        j0 = (kb0 - (qi - 4)) * 128
                        return kb0, nkb, j0, nkb * 128

                    def front(qi):
                        s = qi % 2
                        kb0, nkb, j0, nk = geom(qi)
                        for (a0, a1) in ((0, min(nk, 512)), (512, nk)):
                            if a1 <= a0:
                                continue
                            op("pe", lambda g: g.matmul(pss[:, s, a0:a1], qT[:, qi * 128:(qi + 1) * 128], kT[:, kb0 * 128 + a0:kb0 * 128 + a1],
                                                        start=True, stop=True),
                               reads=["qT", "kT"], writes=[("pss", s, a0)])
                        rk = [("pss", s, 0)] + ([("pss", s, 512)] if nk > 512 else [])
                        op("dve", lambda g: g.scalar_tensor_tensor(st[:, s, :nk], pss[:, s, :nk], scale, bm[:, j0:j0 + nk], ALU.mult, ALU.add),
                           reads=rk + ["bm"], writes=[("s", s)])
                        op("dve", lambda g: g.tensor_reduce(sm[:, s, 1:2], st[:, s, :nk], AX.X, ALU.max, negate=True), reads=[("s", s)], writes=[("sm", s)])
                        op("dve", lambda g: g.memset(sm[:, s, 2:3], 0.0), writes=[("sm2", s)])
                        op("act", lambda g: g.activation(out=pt_[:, s, :nk], in_=st[:, s, :nk], func=AF.Exp, bias=sm[:, s, 1:2], accum_out=sm[:, s, 2:3]),
                           reads=[("s", s), ("sm", s)], writes=[("p", s), ("sm2", s)])
                        op("dve", lambda g: g.reciprocal(sm[:, s, 3:4], sm[:, s, 2:3]), reads=[("sm2", s)], writes=[("sm3", s)])

                    def back(qi):
                        s = qi % 2
                        kb0, nkb, j0, nk = geom(qi)
                        for n in range(nkb):
                            op("pe", lambda g: g.transpose(ptp[:, s, n * 128:(n + 1) * 128], pt_[:, s, n * 128:(n + 1) * 128], ident[:]),
                               reads=[("p", s), "ident"], writes=[("ptp", s)], inc=(n == nkb - 1))
                        if s == 0:
                            op("act", lambda g: g.copy(pT[:, s, :nkb, :].rearrange("p n q -> p (n q)"), ptp[:, s, :nk]), reads=[("ptp", s)], writes=[("pT", s)])
                        else:
                            op("dve", lambda g: g.tensor_copy(pT[:, s, :nkb, :].rearrange("p n q -> p (n q)"), ptp[:, s, :nk]), reads=[("ptp", s)], writes=[("pT", s)])
                        for n in range(nkb):
                            op("pe", lambda g: g.matmul(po[:, s, 0:128], pT[:, s, n, :], vt[:, kb0 + n, :], start=(n == 0), stop=(n == nkb - 1)),
                               reads=[("pT", s), "v"], writes=[("po", s)], inc=(n == nkb - 1))
                        op("dve", lambda g: g.scalar_tensor_tensor(og[:, qi, :], po[:, s, 0:128], sm[:, s, 3:4], zt[:, qi, :], ALU.mult, ALU.mult),
                           reads=[("po", s), ("sm3", s), "z"], writes=["og"])

                    for qi in range(NQ + 1):
                        if qi < NQ:
                            front(qi)
                        if qi > 0:
                            back(qi - 1)
                    for pi in range(S // PS_):
                        tp = tb0 + pi * PS_
                        nsl = slice(pi * (PS_ // 128), (pi + 1) * (PS_ // 128))
                        dma("pool", YN.rows(tp, PS_)[:, h * 128:(h + 1) * 128].rearrange("(n p) d -> p n d", p=128), og[:, nsl, :], reads=["og"], key="ogst")
        cx.barrier()

    import os as _os2
    _nx = int(_os2.environ.get("K_EXTRA_SEMS", "0"))
    if _nx:
        with ExitStack() as es:
            dm = es.enter_context(sbt("dummy", [128, 64], F32))
            for j in range(_nx):
                dma("sp", dm[:, 0:16], x_in[0:128, 0:16], writes=["dummy"], key=("xs_dummy", j))
            cx.barrier()
    op("dve", lambda g: g.memset(epsb[:], c.EPS), writes=["epsb"])
    phase_mod()
    for i in range(n_layers_a):
        phase_convert(ssm_w_in[i], D, IN, 512, W_IN[i])
        phase_convert(ssm_w_out[i], DI, D, 256, W_OUT[i])
    if n_layers_b > 0:
        phase_convert(w_kv, D, 2 * AD, 512, W_KV)
        for i in range(n_layers_b):
            phase_convert(att_w_in[i], D, 2 * AD, 512, W_AIN[i])
            phase_convert(att_w_out[i], AD, D, 512, W_AOUT[i])

    import os as _os
    SKIP = set(_os.environ.get("K_SKIP", "").split(","))
    xcur = x_in
    for i in range(n_layers_a):
        layer = i
        run_gemm("g1", D, IN, 512, W_IN[i], min(1024, S), norm_src_factory(xcur, layer, True), out_store_f32(U))
        if "conv" not in SKIP:
            phase_conv(i)
        if "ssd" not in SKIP:
            phase_ssd(i)
        run_gemm("g2", DI, D, 256, W_OUT[i], min(512, S), load_src_factory(YN, DI), out_store_f32(Y2))
        phase_post(layer, xcur)
        xcur = out
    for i in range(n_layers_b):
        layer = n_layers_a + i
        if i == 0:
            run_gemm("g3", D, 2 * AD, 512, W_KV, min(1024, S), norm_src_factory(xcur, layer, False), out_store_split(KV, 2 * AD, None))
        run_gemm("g4", D, 2 * AD, 512, W_AIN[i], min(1024, S), norm_src_factory(xcur, layer, True), out_store_split(QS, AD, ZS))
        if "attn" not in SKIP:
            phase_attn(i)
        run_gemm("g5", AD, D, 512, W_AOUT[i], min(1024, S), load_src_factory(YN, AD), out_store_f32(Y2))
        phase_post(layer, xcur)
        xcur = out
    cx.barrier()
    return nc


def _bm_index():
    l = np.arange(128)[:, None]
    j = np.arange(640)[None, :]
    dist = l + 512 - j
    return np.clip(dist, -128, 128) + 128


def make_inputs(cfg, inputs):
    c = cfg
    f = lambda a: np.ascontiguousarray(np.asarray(a, dtype=np.float32))
    m = {
        "x": f(inputs["x"]).reshape(c.T, c.D),
        "c": f(inputs["c"]),
        "ada_w": f(inputs["ada_w"]), "ada_b": f(inputs["ada_b"]),
        "pre_norm_g": f(inputs["pre_norm_g"]), "post_norm_g": f(inputs["post_norm_g"]),
        "ssm_w_in": f(inputs["ssm_w_in"]), "ssm_conv_w": f(inputs["ssm_conv_w"]), "ssm_conv_b": f(inputs["ssm_conv_b"]),
        "ssm_dt_bias": f(inputs["ssm_dt_bias"]), "ssm_a_log": f(inputs["ssm_a_log"]), "ssm_d": f(inputs["ssm_d"]),
        "ssm_norm_g": f(inputs["ssm_norm_g"]), "ssm_w_out": f(inputs["ssm_w_out"]),
        "kv_norm_g": f(inputs["kv_norm_g"]), "w_kv": f(inputs["w_kv"]),
        "att_w_in": f(inputs["att_w_in"]), "att_w_out": f(inputs["att_w_out"]),
        "att_bm": f(np.asarray(inputs["att_rel_bias"])[:, :, _bm_index()]),
    }
    return m


def kernel(**inputs):
    nb = int(np.asarray(inputs["x"]).shape[0])
    cfg = Cfg(NB=1)
    nc = build(cfg)
    maps = []
    for b in range(nb):
        ib = dict(inputs)
        ib["x"] = np.asarray(inputs["x"])[b:b + 1]
        ib["c"] = np.asarray(inputs["c"])[b:b + 1]
        maps.append(make_inputs(cfg, ib))
    res = run_bass_kernel_spmd(nc, maps, core_ids=list(range(nb)))
    return np.stack([np.asarray(res.results[b]["out"], dtype=np.float32).reshape(cfg.S, cfg.D) for b in range(nb)], axis=0)
```
